# Optimizing a Trainium2 kernel written in Bass

```python
import math
import jax, jax.numpy as jnp
from jax import lax
import numpy as np

D_MODEL = 2048
BATCH = 16
SEQ = 2048
DEPTH = 4

N_GROUPS = 4
GROUP_W = D_MODEL // N_GROUPS
HEAD_DIM = 128
N_HEADS = GROUP_W // HEAD_DIM
SCONV_K = 3
MLA_Q_RANK = 384
MLA_KV_RANK = 256
MLA_NOPE = 128
MLA_ROPE = 64
MLA_V = 128
ROPE_THETA = 10000.0
ATTN_BLOCK = 128
GDN_CONV_K = 4
CHUNK = 64
D_FF = 4 * D_MODEL
EPS = 1e-6
LB_OFFSET_MAX = 4096

SPLITS = (GROUP_W, GROUP_W, GROUP_W,
          MLA_Q_RANK, MLA_KV_RANK, MLA_ROPE,
          GROUP_W, GROUP_W, GROUP_W, GROUP_W, N_HEADS, N_HEADS,
          GROUP_W, GROUP_W, GROUP_W, GROUP_W)
D_IN = sum(SPLITS)

kernel_name = 'hybrid_parallel_heads_conv_mla_gdn_hgrn2'


def rms_norm(x, g):
    xf = x.astype(jnp.float32)
    y = xf * lax.rsqrt(jnp.mean(xf * xf, axis=-1, keepdims=True) + EPS)
    return (y * g.astype(jnp.float32)).astype(x.dtype)


def l2_norm(x):
    return x * lax.rsqrt(jnp.sum(x * x, axis=-1, keepdims=True) + EPS)


def split_cols(proj):
    out, start = [], 0
    for w in SPLITS:
        out.append(proj[..., start:start + w])
        start += w
    return out


def causal_dwconv(x, w):
    k, c = w.shape
    return lax.conv_general_dilated(x, w[:, None, :].astype(x.dtype), window_strides=(1,),
                                    padding=[(k - 1, 0)], dimension_numbers=('NWC', 'WIO', 'NWC'),
                                    feature_group_count=c)


def rope(x, cos, sin):
    half = x.shape[-1] // 2
    xf = x.astype(jnp.float32)
    x1, x2 = xf[..., :half], xf[..., half:]
    return jnp.concatenate([x1 * cos - x2 * sin, x2 * cos + x1 * sin], axis=-1).astype(x.dtype)


def to_chunks(t):
    b, s, h = t.shape[:3]
    t = t.reshape((b, s // CHUNK, CHUNK, h) + t.shape[3:])
    return t.transpose((1, 0, 3, 2) + tuple(range(4, t.ndim)))


def from_chunks(t):
    n, b, h, c, d = t.shape
    return t.transpose(1, 0, 3, 2, 4).reshape(b, n * c, h, d)


def mla_attention(cq, ckv, kr, cos, sin, gq, gkv, w_uq, w_ukv):
    b, s, _ = cq.shape
    q = (rms_norm(cq, gq) @ w_uq).reshape(b, s, N_HEADS, MLA_NOPE + MLA_ROPE)
    kv = (rms_norm(ckv, gkv) @ w_ukv).reshape(b, s, N_HEADS, MLA_NOPE + MLA_V)
    q = jnp.concatenate([q[..., :MLA_NOPE], rope(q[..., MLA_NOPE:], cos[:, :, None], sin[:, :, None])], axis=-1)
    kr = rope(kr, cos, sin)[:, :, None, :]
    k = jnp.concatenate([kv[..., :MLA_NOPE], jnp.broadcast_to(kr, (b, s, N_HEADS, MLA_ROPE))], axis=-1)
    v = kv[..., MLA_NOPE:]
    scale = (MLA_NOPE + MLA_ROPE) ** -0.5
    nb = s // ATTN_BLOCK
    qb = q.reshape(b, nb, ATTN_BLOCK, N_HEADS, -1).transpose(1, 0, 3, 2, 4)
    kpos = jnp.arange(s)

    def block(args):
        qi, i = args
        sc = jnp.einsum('bhqd,bkhd->bhqk', qi, k, preferred_element_type=jnp.float32) * scale
        qpos = i * ATTN_BLOCK + jnp.arange(ATTN_BLOCK)
        sc = jnp.where(kpos[None, :] <= qpos[:, None], sc, -jnp.inf)
        p = jax.nn.softmax(sc, axis=-1).astype(v.dtype)
        return jnp.einsum('bhqk,bkhd->bqhd', p, v)

    o = lax.map(block, (qb, jnp.arange(nb)))
    return o.transpose(1, 0, 2, 3, 4).reshape(b, s, N_HEADS * MLA_V)


def gated_delta_rule(q, k, v, log_a, beta):
    b, s, h, dk = q.shape
    dv = v.shape[-1]
    q = l2_norm(q) * (dk ** -0.5)
    k = l2_norm(k)
    tri = jnp.tril(jnp.ones((CHUNK, CHUNK), dtype=bool))
    strict = jnp.tril(jnp.ones((CHUNK, CHUNK), dtype=bool), k=-1)
    eye = jnp.eye(CHUNK, dtype=jnp.float32)

    def step(state, inp):
        qc, kc, vc, gc, bc = inp
        gcum = jnp.cumsum(gc, axis=-1)
        decay = jnp.exp(jnp.where(tri, gcum[..., :, None] - gcum[..., None, :], -jnp.inf))
        kb = kc * bc[..., None]
        lower = jnp.where(strict, jnp.einsum('bhtd,bhsd->bhts', kb, kc) * decay, 0.0)
        rhs = jnp.concatenate([vc * bc[..., None], kb * jnp.exp(gcum)[..., None]], axis=-1)
        sol = lax.linalg.triangular_solve(eye + lower, rhs, left_side=True, lower=True, unit_diagonal=True)
        u, w = sol[..., :dv], sol[..., dv:]
        v_new = u - jnp.einsum('bhck,bhkv->bhcv', w, state)
        attn = jnp.einsum('bhtd,bhsd->bhts', qc, kc) * decay
        out = (jnp.einsum('bhtk,bhkv->bhtv', qc * jnp.exp(gcum)[..., None], state)
               + jnp.einsum('bhts,bhsv->bhtv', attn, v_new))
        g_last = gcum[..., -1:]
        state = (state * jnp.exp(g_last)[..., None]
                 + jnp.einsum('bhck,bhcv->bhkv', kc * jnp.exp(g_last - gcum)[..., None], v_new))
        return state, out

    state0 = jnp.zeros((b, h, dk, dv), jnp.float32)
    _, o = lax.scan(step, state0, (to_chunks(q), to_chunks(k), to_chunks(v), to_chunks(log_a), to_chunks(beta)))
    return from_chunks(o)


def hgrn2_recurrence(q, k, i, log_f):
    b, s, h, dk = q.shape
    dv = i.shape[-1]
    tri = jnp.tril(jnp.ones((CHUNK, CHUNK), dtype=bool))[:, :, None]

    def step(state, inp):
        qc, kc, ic, gc = inp
        bcum = jnp.cumsum(gc, axis=2)
        dec = jnp.exp(jnp.where(tri, bcum[:, :, :, None, :] - bcum[:, :, None, :, :], -jnp.inf))
        attn = jnp.einsum('bhtd,bhsd,bhtsd->bhts', qc, kc, dec)
        out = (jnp.einsum('bhtk,bhkv->bhtv', qc * jnp.exp(bcum), state)
               + jnp.einsum('bhts,bhsv->bhtv', attn, ic))
        b_last = bcum[:, :, -1:, :]
        state = (state * jnp.exp(b_last[:, :, 0, :])[..., None]
                 + jnp.einsum('bhck,bhcv->bhkv', kc * jnp.exp(b_last - bcum), ic))
        return state, out

    state0 = jnp.zeros((b, h, dk, dv), jnp.float32)
    _, o = lax.scan(step, state0, (to_chunks(q), to_chunks(k), to_chunks(i), to_chunks(log_f)))
    return from_chunks(o)


def setup_inputs(seed: int = 0) -> dict:
    key = jax.random.key(seed)
    ks = jax.random.split(key, 24)
    f32 = jnp.float32
    L = DEPTH

    def nrm(k, shape, scale):
        return jax.random.normal(k, shape, f32) * scale

    def gain(k, shape):
        return 1.0 + 0.02 * jax.random.normal(k, shape, f32)

    x = nrm(ks[0], (BATCH, SEQ, D_MODEL), 1.0)
    positions = (jax.random.randint(ks[1], (BATCH, 1), 0, LB_OFFSET_MAX, dtype=jnp.int32)
                 + jnp.arange(SEQ, dtype=jnp.int32)[None, :])
    dt = jnp.exp(jax.random.uniform(ks[13], (L, N_HEADS), f32, math.log(1e-3), math.log(1e-1)))
    return {
        'x': x,
        'positions': positions,
        'norm1_g': gain(ks[2], (L, D_MODEL)),
        'w_in': nrm(ks[3], (L, D_MODEL, D_IN), D_MODEL ** -0.5),
        'sconv_w': nrm(ks[4], (L, SCONV_K, GROUP_W), SCONV_K ** -0.5),
        'sconv_out_g': gain(ks[5], (L, GROUP_W)),
        'mla_q_g': gain(ks[6], (L, MLA_Q_RANK)),
        'mla_kv_g': gain(ks[7], (L, MLA_KV_RANK)),
        'mla_w_uq': nrm(ks[8], (L, MLA_Q_RANK, N_HEADS * (MLA_NOPE + MLA_ROPE)), MLA_Q_RANK ** -0.5),
        'mla_w_ukv': nrm(ks[9], (L, MLA_KV_RANK, N_HEADS * (MLA_NOPE + MLA_V)), MLA_KV_RANK ** -0.5),
        'mla_out_g': gain(ks[10], (L, GROUP_W)),
        'gdn_conv_w': nrm(ks[11], (L, GDN_CONV_K, 3 * GROUP_W), GDN_CONV_K ** -0.5),
        'gdn_a_log': jnp.log(jax.random.uniform(ks[12], (L, N_HEADS), f32, 1.0, 16.0)),
        'gdn_dt_bias': dt + jnp.log(-jnp.expm1(-dt)),
        'gdn_norm_g': gain(ks[14], (L, HEAD_DIM)),
        'hgrn_lb_logits': nrm(ks[15], (L, GROUP_W), 0.1),
        'hgrn_norm_g': gain(ks[16], (L, HEAD_DIM)),
        'w_o': nrm(ks[17], (L, N_GROUPS * GROUP_W, D_MODEL), (N_GROUPS * GROUP_W) ** -0.5),
        'norm2_g': gain(ks[18], (L, D_MODEL)),
        'w_ff1': nrm(ks[19], (L, D_MODEL, D_FF), D_MODEL ** -0.5),
        'w_ff2': nrm(ks[20], (L, D_FF, D_MODEL), D_FF ** -0.5),
        'final_g': gain(ks[21], (D_MODEL,)),
    }


def reference(x, positions, norm1_g, w_in, sconv_w, sconv_out_g, mla_q_g, mla_kv_g, mla_w_uq, mla_w_ukv,
              mla_out_g, gdn_conv_w, gdn_a_log, gdn_dt_bias, gdn_norm_g, hgrn_lb_logits, hgrn_norm_g,
              w_o, norm2_g, w_ff1, w_ff2, final_g):
    f32 = jnp.float32
    b, s, _ = x.shape
    half = MLA_ROPE // 2
    inv_freq = ROPE_THETA ** (-jnp.arange(half, dtype=f32) / half)
    ang = positions.astype(f32)[..., None] * inv_freq
    cos, sin = jnp.cos(ang), jnp.sin(ang)
    p_lb = jax.nn.softmax(hgrn_lb_logits.astype(f32), axis=0)
    lower_bounds = jnp.cumsum(p_lb, axis=0) - p_lb[0]

    def heads(t):
        return t.astype(f32).reshape(b, s, N_HEADS, HEAD_DIM)

    h = x
    for l in range(DEPTH):
        u = rms_norm(h, norm1_g[l])
        (sc_x, sc_c, sc_b, m_cq, m_ckv, m_kr,
         g_q, g_k, g_v, g_z, g_a, g_b,
         r_q, r_f, r_i, r_z) = split_cols(u @ w_in[l])

        y_sc = rms_norm(sc_b * causal_dwconv(sc_c * sc_x, sconv_w[l]), sconv_out_g[l])

        y_mla = rms_norm(mla_attention(m_cq, m_ckv, m_kr, cos, sin, mla_q_g[l], mla_kv_g[l],
                                       mla_w_uq[l], mla_w_ukv[l]), mla_out_g[l])

        qkv = jax.nn.silu(causal_dwconv(jnp.concatenate([g_q, g_k, g_v], axis=-1), gdn_conv_w[l]))
        qkv = qkv.astype(f32).reshape(b, s, 3, N_HEADS, HEAD_DIM)
        beta = jax.nn.sigmoid(g_b.astype(f32))
        log_a = -jnp.exp(gdn_a_log[l].astype(f32)) * jax.nn.softplus(g_a.astype(f32) + gdn_dt_bias[l].astype(f32))
        o_gdn = gated_delta_rule(qkv[:, :, 0], qkv[:, :, 1], qkv[:, :, 2], log_a, beta)
        y_gdn = (rms_norm(o_gdn, gdn_norm_g[l]) * jax.nn.silu(heads(g_z))).reshape(b, s, GROUP_W).astype(x.dtype)

        lb = lower_bounds[l].reshape(N_HEADS, HEAD_DIM)
        fr = heads(r_f)
        log_f = jnp.log(lb + (1.0 - lb) * jax.nn.sigmoid(fr))
        k_in = (1.0 - lb) * jax.nn.sigmoid(-fr)
        o_hg = hgrn2_recurrence(jax.nn.silu(heads(r_q)), k_in, heads(r_i), log_f)
        y_hg = (rms_norm(o_hg, hgrn_norm_g[l]) * jax.nn.sigmoid(heads(r_z))).reshape(b, s, GROUP_W).astype(x.dtype)

        h = h + jnp.concatenate([y_sc, y_mla, y_gdn, y_hg], axis=-1) @ w_o[l]

        v_ff = rms_norm(h, norm2_g[l])
        h = h + jnp.square(jax.nn.relu(v_ff @ w_ff1[l])) @ w_ff2[l]

    return rms_norm(h, final_g)
```

```python
import math
import numpy as np
import concourse.bass as bass
import concourse.mybir as mybir
from concourse.bass_utils import run_bass_kernel_spmd

F32 = mybir.dt.float32
BF16 = mybir.dt.bfloat16
I32 = mybir.dt.int32
AF = mybir.ActivationFunctionType
ALU = mybir.AluOpType
AX = mybir.AxisListType

D = 2048
S = 2048
L = 4
DIN = 6344
DFF = 8192
NCORES = 8
SEQ_PER_CORE = 2
EPS = 1e-6

_ESZ = {F32: 4, BF16: 2, I32: 4}


def _prod(xs):
    r = 1
    for x in xs:
        r *= int(x)
    return r


class Op:
    __slots__ = ("eng", "fn", "deps", "sig", "sem", "val", "is_dma", "prev_slot")

    def __init__(self, eng, fn):
        self.eng = eng
        self.fn = fn
        self.deps = set()
        self.sig = False
        self.sem = None
        self.val = 0
        self.is_dma = False
        self.prev_slot = None


class Prog:
    ENGS = ("pe", "act", "dve", "pool", "sp")
    RING = {"sp": 40, "act": 16, "pool": 16}

    def __init__(self, nc):
        self.nc = nc
        self.ops = {e: [] for e in self.ENGS}
        self.track = {}
        self.untracked = set()
        self.dma_n = {q: 0 for q in self.RING}
        self.ring_last = {q: [None] * n for q, n in self.RING.items()}
        self.all_dmas = []
        self.extra_dep = None

    def region(self, ap):
        name = ap.name
        if name in self.untracked:
            return None
        t = ap.tensor
        shape = tuple(t.shape)
        esz = _ESZ[ap.dtype]
        rowlen = _prod(shape[1:])
        off = int(ap.offset)
        pat = ap.ap
        r0 = off // rowlen
        c0 = off % rowlen
        if type(t).__name__.startswith("PSum"):
            return (name, 0, 128, 0, 2048)
        if type(t).__name__.startswith("DRam"):
            rext = 0
            cext = 0
            for (st, cnt) in pat:
                if cnt <= 1:
                    continue
                if st % rowlen == 0:
                    rext += (st // rowlen) * (cnt - 1)
                else:
                    cext += st * (cnt - 1)
            if c0 + cext + 1 > rowlen:
                rext += (c0 + cext) // rowlen
                c0, cext = 0, rowlen - 1
            return (name, r0, r0 + rext + 1, c0 * esz, (c0 + cext + 1) * esz)
        pcnt = pat[0][1]
        fext = 0
        for (st, cnt) in pat[1:]:
            if cnt > 1:
                fext += st * (cnt - 1)
        return (name, r0, r0 + pcnt, c0 * esz, (c0 + fext + 1) * esz)

    def _deps_and_record(self, op, ap, is_write):
        reg = self.region(ap)
        if reg is None:
            return
        name, p0, p1, b0, b1 = reg
        psum = type(ap.tensor).__name__.startswith("PSum")
        recs = self.track.setdefault(name, [])
        keep = []
        for rec in recs:
            q0, q1, c0, c1, kind, rop = rec
            ov = (p0 < q1 and q0 < p1 and b0 < c1 and c0 < b1)
            if ov and (is_write or kind == "W" or (psum and rop.eng != op.eng)):
                if rop is not op:
                    op.deps.add(rop)
            if is_write and ov and q0 >= p0 and q1 <= p1 and c0 >= b0 and c1 <= b1:
                continue
            if (not is_write) and kind == "R" and rop.eng == op.eng and (q0, q1, c0, c1) == (p0, p1, b0, b1) \
                    and not rop.is_dma and not op.is_dma:
                continue
            keep.append(rec)
        keep.append((p0, p1, b0, b1, "W" if is_write else "R", op))
        self.track[name] = keep

    def add(self, eng, fn, reads=(), writes=()):
        op = Op(eng, fn)
        for ap in reads:
            if ap is not None and not isinstance(ap, (int, float)):
                self._deps_and_record(op, ap, False)
        for ap in writes:
            if ap is not None:
                self._deps_and_record(op, ap, True)
        if eng == "pe":
            op.deps = {d for d in op.deps if d.eng != "pe" or d.is_dma}
        self.ops[eng].append(op)
        return op

    def dma(self, out, in_, q="sp"):
        op = Op(q, lambda e, o=out, i=in_: e.dma_start(out=o, in_=i))
        op.is_dma = True
        n = self.dma_n[q]
        self.dma_n[q] = n + 1
        slot = n % self.RING[q]
        op.sem = (q, slot)
        op.val = 16 * (n // self.RING[q] + 1)
        op.prev_slot = self.ring_last[q][slot]
        self.ring_last[q][slot] = op
        if op.prev_slot is not None:
            op.deps.add(op.prev_slot)
        if self.extra_dep is not None:
            op.deps.add(self.extra_dep)
        self._deps_and_record(op, in_, False)
        self._deps_and_record(op, out, True)
        self.ops[q].append(op)
        self.all_dmas.append(op)
        return op

    def mm(self, out, lhsT, rhs, start=True, stop=True):
        return self.add("pe", lambda e: e.matmul(out, lhsT, rhs, start=start, stop=stop),
                        reads=(lhsT, rhs), writes=(out,))

    def transpose(self, out, in_, ident):
        return self.add("pe", lambda e: e.transpose(out, in_, ident), reads=(in_, ident), writes=(out,))

    def act(self, out, in_, func, bias=None, scale=1.0, accum_out=None, eng="act"):
        def fn(e):
            kw = {}
            if bias is not None:
                kw["bias"] = bias
            if accum_out is not None:
                kw["accum_out"] = accum_out
            return e.activation(out=out, in_=in_, func=func, scale=scale, **kw)
        rd = [in_]
        if bias is not None and not isinstance(bias, (int, float)):
            rd.append(bias)
        if not isinstance(scale, (int, float)):
            rd.append(scale)
        wr = [out]
        if accum_out is not None:
            wr.append(accum_out)
        return self.add(eng, fn, reads=rd, writes=wr)

    def ts(self, eng, out, in0, s1, s2, op0, op1=None):
        def fn(e):
            if op1 is None:
                return e.tensor_scalar(out=out, in0=in0, scalar1=s1, scalar2=None, op0=op0)
            return e.tensor_scalar(out=out, in0=in0, scalar1=s1, scalar2=s2, op0=op0, op1=op1)
        rd = [in0] + [s for s in (s1, s2) if s is not None and not isinstance(s, (int, float))]
        return self.add(eng, fn, reads=rd, writes=(out,))

    def tt(self, eng, out, in0, in1, op):
        return self.add(eng, lambda e: e.tensor_tensor(out=out, in0=in0, in1=in1, op=op),
                        reads=(in0, in1), writes=(out,))

    def stt(self, out, in0, scalar, in1, op0, op1):
        rd = [in0, in1] + ([] if isinstance(scalar, (int, float)) else [scalar])
        return self.add("dve", lambda e: e.scalar_tensor_tensor(out=out, in0=in0, scalar=scalar, in1=in1,
                                                                op0=op0, op1=op1), reads=rd, writes=(out,))

    def copy(self, eng, out, in_):
        if eng == "act":
            return self.add("act", lambda e: e.copy(out=out, in_=in_), reads=(in_,), writes=(out,))
        return self.add(eng, lambda e: e.tensor_copy(out=out, in_=in_), reads=(in_,), writes=(out,))

    def memset(self, eng, out, val):
        return self.add(eng, lambda e: e.memset(out, val), reads=(), writes=(out,))

    def emit(self, semctx):
        nc = self.nc
        fin = Op("sp", None)
        for q in self.RING:
            for o in self.ring_last[q]:
                if o is not None:
                    fin.deps.add(o)
        self.ops["sp"].append(fin)
        for e in self.ENGS:
            for op in self.ops[e]:
                for d in op.deps:
                    d.sig = True
        esem = {}
        for e in ("pe", "act", "dve", "pool"):
            esem[e] = semctx(f"sem_{e}")
        rsem = {}
        for q, n in self.RING.items():
            for i in range(n):
                rsem[(q, i)] = semctx(f"ring_{q}_{i}")
        for e in ("pe", "act", "dve", "pool"):
            cnt = 0
            for op in self.ops[e]:
                if op.is_dma:
                    op.sem = rsem[op.sem]
                elif op.sig:
                    cnt += 1
                    op.sem = esem[e]
                    op.val = cnt
        for op in self.ops["sp"]:
            if op.is_dma:
                op.sem = rsem[op.sem]
        self.n_inst = {e: len(self.ops[e]) for e in self.ENGS}

        def run(eng_name, e):
            waited = {}
            for op in self.ops[eng_name]:
                needs = {}
                for d in op.deps:
                    k = id(d.sem)
                    if k not in needs or needs[k][1] < d.val:
                        needs[k] = (d.sem, d.val)
                for k, (sem, val) in needs.items():
                    if waited.get(k, 0) < val:
                        e.wait_ge(sem, val)
                        waited[k] = val
                if op.fn is None:
                    continue
                ins = op.fn(e)
                if op.is_dma:
                    ins.then_inc(op.sem, 16)
                elif op.sig:
                    ins.then_inc(op.sem, 1)

        with nc.Block() as block:
            @block.tensor
            def _(e):
                run("pe", e)

            @block.scalar
            def _(e):
                run("act", e)

            @block.vector
            def _(e):
                run("dve", e)

            @block.gpsimd
            def _(e):
                run("pool", e)

            @block.sync
            def _(e):
                run("sp", e)


def _fm_chunks():
    ch = []
    def seg(name, c0, n):
        for i in range(n // 128):
            ch.append((name, [(c0 + i * 128, 128)]))
    seg("sc_x", 0, 512); seg("sc_c", 512, 512); seg("sc_b", 1024, 512)
    seg("m_cq", 1536, 384); seg("m_ckv", 1920, 256)
    ch.append(("m_kr", [(2176, 64)]))
    ch.append(("m_krs", [(2208, 32), (2176, 32)]))
    seg("g_q", 2240, 512); seg("g_k", 2752, 512); seg("g_v", 3264, 512); seg("g_z", 3776, 512)
    seg("r_q", 4296, 512); seg("r_f", 4808, 512); seg("r_z", 5832, 512)
    return ch

FM = _fm_chunks()
NFM = len(FM)
FMI = {}
for _j, (_n, _s) in enumerate(FM):
    FMI.setdefault(_n, _j)
COL_RI = 5320
COL_GAB = 4288
NTM = 520

PV = {}
_o = 0
for _n, _w in (("g1", 16), ("g2", 16), ("scw", 12), ("scg", 4), ("mqg", 3), ("mkvg", 2), ("mog", 4),
               ("gcw", 48), ("gng", 1), ("hng", 1), ("alog", 4), ("dtb", 4)):
    PV[_n] = _o
    _o += _w
NV = _o

CO = {}
_o = 0
for _n, _w in (("ID", 128), ("TRI", 64), ("MU8", 512), ("NEGS8", 512), ("NEGM8", 512), ("I8", 512),
               ("CAUS", 128), ("ONES", 128), ("INVF", 1), ("SIGN", 1), ("SCANM", 2048)):
    CO[_n] = _o
    _o += _w
NCONST = _o


def make_consts():
    c = np.zeros((128, NCONST), np.float32)
    c[:, CO["ID"]:CO["ID"] + 128] = np.eye(128, dtype=np.float32)
    j = np.arange(64)[:, None]
    t = np.arange(64)[None, :]
    c[:64, CO["TRI"]:CO["TRI"] + 64] = (j <= t).astype(np.float32)
    mu = (j <= t).astype(np.float32)
    negs = -(j < t).astype(np.float32)
    negm = np.where(j <= t, 0.0, -30000.0).astype(np.float32)
    c[:64, CO["MU8"]:CO["MU8"] + 512] = np.tile(mu, (1, 8))
    c[:64, CO["NEGS8"]:CO["NEGS8"] + 512] = np.tile(negs, (1, 8))
    c[:64, CO["NEGM8"]:CO["NEGM8"] + 512] = np.tile(negm, (1, 8))
    c[:64, CO["I8"]:CO["I8"] + 512] = np.tile(np.eye(64, dtype=np.float32), (1, 8))
    k = np.arange(128)[:, None]
    q = np.arange(128)[None, :]
    c[:, CO["CAUS"]:CO["CAUS"] + 128] = (k <= q).astype(np.float32)
    c[:, CO["ONES"]:CO["ONES"] + 128] = 1.0
    half = 32
    invf = (10000.0 ** (-np.arange(half, dtype=np.float32) / half)).astype(np.float32)
    c[:64, CO["INVF"]] = np.concatenate([invf, invf])
    c[:64, CO["SIGN"]] = np.concatenate([-np.ones(32), np.ones(32)]).astype(np.float32)
    sm = np.ones(2048, np.float32)
    sm[::64] = 0.0
    c[:, CO["SCANM"]:CO["SCANM"] + 2048] = sm[None, :]
    return c


def colmaj(v, nchunk):
    v = np.asarray(v, np.float32)
    return np.ascontiguousarray(np.swapaxes(v.reshape(v.shape[:-1] + (nchunk, 128)), -1, -2))


def make_pvec(inp):
    pv = np.zeros((L, 128, NV), np.float32)
    for l in range(L):
        pv[l, :, PV["g1"]:PV["g1"] + 16] = colmaj(inp["norm1_g"][l], 16)
        pv[l, :, PV["g2"]:PV["g2"] + 16] = colmaj(inp["norm2_g"][l], 16)
        for j in range(3):
            pv[l, :, PV["scw"] + j * 4:PV["scw"] + j * 4 + 4] = colmaj(inp["sconv_w"][l, j], 4)
        pv[l, :, PV["scg"]:PV["scg"] + 4] = colmaj(inp["sconv_out_g"][l], 4)
        pv[l, :, PV["mqg"]:PV["mqg"] + 3] = colmaj(inp["mla_q_g"][l], 3)
        pv[l, :, PV["mkvg"]:PV["mkvg"] + 2] = colmaj(inp["mla_kv_g"][l], 2)
        pv[l, :, PV["mog"]:PV["mog"] + 4] = colmaj(inp["mla_out_g"][l], 4)
        for j in range(4):
            pv[l, :, PV["gcw"] + j * 12:PV["gcw"] + j * 12 + 12] = colmaj(inp["gdn_conv_w"][l, j], 12)
        pv[l, :, PV["gng"]] = np.asarray(inp["gdn_norm_g"][l], np.float32)
        pv[l, :, PV["hng"]] = np.asarray(inp["hgrn_norm_g"][l], np.float32)
        pv[l, :, PV["alog"]:PV["alog"] + 4] = np.asarray(inp["gdn_a_log"][l], np.float32)[None, :]
        pv[l, :, PV["dtb"]:PV["dtb"] + 4] = np.asarray(inp["gdn_dt_bias"][l], np.float32)[None, :]
    return pv


NAR = 48000


class Arena:
    def __init__(self, ar):
        self.ar = ar
        self.off = 0

    def reset(self, off=0):
        self.off = off

    def f(self, *shape):
        n = _prod(shape)
        v = self.ar[:, self.off:self.off + n]
        self.off += n
        assert self.off <= NAR, ("arena overflow", self.off)
        return self._shape(v, shape)

    def h(self, *shape):
        n = _prod(shape)
        nf = (n + 1) // 2
        v = self.ar[:, self.off:self.off + nf].bitcast(BF16)
        if 2 * nf != n:
            v = v[:, 0:n]
        self.off += nf
        assert self.off <= NAR, ("arena overflow", self.off)
        return self._shape(v, shape)

    @staticmethod
    def _shape(v, shape):
        if len(shape) == 1:
            return v
        if len(shape) == 2:
            return v.rearrange("p (a b) -> p a b", b=shape[1])
        if len(shape) == 3:
            return v.rearrange("p (a b c) -> p a b c", b=shape[1], c=shape[2])
        raise ValueError(shape)


class Builder:
    def __init__(self, nc, es, n_layers=L, n_seq=SEQ_PER_CORE, debug=None, phases=None):
        self.nc = nc
        self.es = es
        self.nl = n_layers
        self.ns = n_seq
        self.debug = debug or {}
        self.phases = phases
        self.P = Prog(nc)
        P = self.P
        ntok = n_seq * S
        self.ntok = ntok
        dt = nc.dram_tensor
        self.x = dt("x", [ntok, D], F32, kind="ExternalInput").ap()
        self.pos = dt("pos", [n_seq, S], I32, kind="ExternalInput").ap()
        self.w_in = dt("w_in", [L, D, DIN], F32, kind="ExternalInput").ap()
        self.w_o = dt("w_o", [L, D, D], F32, kind="ExternalInput").ap()
        self.w_ff1 = dt("w_ff1", [L, D, DFF], F32, kind="ExternalInput").ap()
        self.w_ff2 = dt("w_ff2", [L, DFF, D], F32, kind="ExternalInput").ap()
        self.w_uq = dt("w_uq", [L, 384, 768], F32, kind="ExternalInput").ap()
        self.w_ukv = dt("w_ukv", [L, 256, 1024], F32, kind="ExternalInput").ap()
        self.pvec = dt("pvec", [L, 128, NV], F32, kind="ExternalInput").ap()
        self.gbc = dt("gbc", [2 * L + 1, D], F32, kind="ExternalInput").ap()
        self.lbl = dt("lbl", [128, 16], F32, kind="ExternalInput").ap()
        self.consts = dt("consts", [128, NCONST], F32, kind="ExternalInput").ap()
        for n in ("x", "pos", "w_in", "w_o", "w_ff1", "w_ff2", "w_uq", "w_ukv", "pvec", "gbc", "lbl", "consts"):
            P.untracked.add(n)
        self.out = dt("out", [ntok, D], F32, kind="ExternalOutput").ap()
        def scr(name, shape, dtype):
            kind = "ExternalOutput" if name in self.debug else "Internal"
            return dt(name, shape, dtype, kind=kind).ap()
        self.H = scr("H", [ntok, D], F32)
        self.PT = scr("PT", [NFM * 128, S], F32)
        self.PNri = scr("PNri", [S, 512], F32)
        self.PNg = scr("PNg", [S, 8], F32)
        self.YT = scr("YT", [D, S], BF16)
        self.ROPE = scr("ROPE", [n_seq * 2 * 64, S], F32)
        self.WIN = [scr(f"WIN{l}", [NFM, 128, 2048], BF16) for l in range(n_layers)]
        self.WTM = [scr(f"WTM{l}", [128, 16 * NTM], BF16) for l in range(n_layers)]
        self.W1 = [scr(f"W1_{l}", [64, 128, 2048], BF16) for l in range(n_layers)]
        self.W2 = [scr(f"W2_{l}", [4 * 16, 128, 2048], BF16) for l in range(n_layers)]
        self.WO = [scr(f"WO_{l}", [4 * 4, 128, 2048], BF16) for l in range(n_layers)]
        sb = lambda name, shape, dtype: es.enter_context(nc.sbuf_tensor(name, shape, dtype))
        self.C32 = sb("C32", [128, NCONST - 2048], F32)
        self.C16 = sb("C16", [128, 384 + 2048], BF16)
        self.PVs = sb("PVs", [128, NV], F32)
        self.LB = sb("LB", [128, 49], F32)
        self.AR = sb("AR", [128, NAR], F32)
        self.ar = Arena(self.AR)
        self.ps = [es.enter_context(nc.psum_tensor(f"ps{i}", [128, 512], F32)) for i in range(8)]
        self._bank = 0
        self._eng_rr = 0

    def c32(self, name, n, rows=128):
        return self.C32[0:rows, CO[name]:CO[name] + n]

    def pv(self, name, i=0, n=1):
        return self.PVs[:, PV[name] + i:PV[name] + i + n]

    def nb(self):
        b = self.ps[self._bank]
        self._bank = (self._bank + 1) % 8
        return b

    def rr(self, engs=("act", "dve")):
        self._eng_rr += 1
        return engs[self._eng_rr % len(engs)]

    def evac(self, out, in_, eng=None):
        eng = eng or self.rr()
        return self.P.copy(eng, out, in_)

    def setup(self):
        P = self.P
        P.dma(self.C32[:, :], self.consts[:, 0:NCONST - 2048])
        ar = self.ar
        ar.reset()
        tmp = ar.f(2048)
        P.dma(tmp, self.consts[:, CO["SCANM"]:CO["SCANM"] + 2048])
        self.ID16 = self.C16[:, 0:128]
        self.ONES16 = self.C16[:, 128:256]
        self.CAUS16 = self.C16[:, 256:384]
        self.SCANM16 = self.C16[:, 384:384 + 2048]
        P.copy("dve", self.ID16, self.c32("ID", 128))
        P.copy("dve", self.ONES16, self.c32("ONES", 128))
        P.copy("dve", self.CAUS16, self.c32("CAUS", 128))
        P.copy("dve", self.SCANM16, tmp)
        self.ID32 = self.c32("ID", 128)
        lbt = ar.f(16)
        P.dma(lbt, self.lbl[:, :])
        e = ar.f(16)
        P.act(e, lbt, AF.Exp)
        s = ar.f(4)
        P.add("dve", lambda en: en.tensor_reduce(out=s, in_=e.rearrange("p (c l) -> p c l", l=4),
                                                 axis=AX.X, op=ALU.add), reads=(e,), writes=(s,))
        rs = ar.f(4)
        P.add("dve", lambda en: en.reciprocal(out=rs, in_=s), reads=(s,), writes=(rs,))
        pl = ar.f(16)
        for c in range(4):
            P.ts("dve", pl[:, c * 4:c * 4 + 4], e[:, c * 4:c * 4 + 4], rs[:, c:c + 1], None, ALU.mult)
        lb = self.LB[:, 0:16].rearrange("p (c l) -> p c l", l=4)
        pl3 = pl.rearrange("p (c l) -> p c l", l=4)
        P.memset("dve", lb[:, :, 0:1], 0.0)
        for l in range(1, 4):
            P.tt("dve", lb[:, :, l:l + 1], lb[:, :, l - 1:l], pl3[:, :, l:l + 1], ALU.add)
        P.ts("dve", self.LB[:, 16:32], self.LB[:, 0:16], -1.0, 1.0, ALU.mult, ALU.add)
        P.ts("dve", self.LB[:, 32:48], self.LB[:, 0:16], 1.0, -1.0, ALU.mult, ALU.add)
        self.EPSC = self.LB[:, 48:49]
        P.memset("dve", self.EPSC, EPS)

    def load_layer_params(self, l):
        self.P.dma(self.PVs[:, :], self.pvec[l])

    def convert_items(self, l):
        items = []
        P = self.P
        w_in = self.w_in[l].rearrange("(kc p) c -> p kc c", p=128)
        for j, (_, segs) in enumerate(FM):
            def it(j=j, segs=segs):
                dst = self.WIN[l][j].rearrange("p (kc c) -> p kc c", c=128)
                o = 0
                for (c0, n) in segs:
                    P.dma(dst[:, :, o:o + n], w_in[:, :, c0:c0 + n], q="pool")
                    o += n
            items.append(it)
        wtm = self.WTM[l].rearrange("p (kc c) -> p kc c", c=NTM)
        for c_lo in range(0, 512, 128):
            items.append(lambda c_lo=c_lo: P.dma(wtm[:, :, c_lo:c_lo + 128], w_in[:, :, COL_RI + c_lo:COL_RI + c_lo + 128], q="pool"))
        items.append(lambda: P.dma(wtm[:, :, 512:520], w_in[:, :, COL_GAB:COL_GAB + 8], q="pool"))
        w1 = self.w_ff1[l].rearrange("(kc p) c -> p kc c", p=128)
        for f in range(64):
            items.append(lambda f=f: P.dma(self.W1[l][f].rearrange("p (kc c) -> p kc c", c=128),
                                           w1[:, :, f * 128:(f + 1) * 128], q="pool"))
        for (src, dst, nchunk) in ((self.w_ff2[l], self.W2[l], 64), (self.w_o[l], self.WO[l], 16)):
            nfg = nchunk // 4
            dstv = dst.rearrange("(q fg) p (fi c) -> p q fg fi c", fg=nfg, c=512)
            for f in range(nchunk):
                items.append(lambda f=f, src=src, dstv=dstv: P.dma(
                    dstv[:, :, f // 4, f % 4, :], src[f * 128:(f + 1) * 128, :].rearrange("p (q c) -> p q c", c=512), q="pool"))
        return items

    def convert_now(self, items):
        for it in items:
            it()

    def pump_alloc(self):
        pass

    def pump(self, n=1, after=None):
        for _ in range(n):
            if not self.convq:
                return
            self.P.extra_dep = after
            self.convq.pop(0)()
            self.P.extra_dep = None

    def norm_to_fm(self, h32, gbc, hn2, junk, ss, outT):
        P = self.P
        for b in range(4):
            P.act(junk, h32[:, b, :], AF.Square, accum_out=ss[:, b:b + 1])
        P.ts("dve", ss[:, 4:8], ss[:, 0:4], 1.0 / D, EPS, ALU.mult, ALU.add)
        P.act(ss[:, 4:8], ss[:, 4:8], AF.Sqrt)
        P.add("dve", lambda e: e.reciprocal(out=ss[:, 8:12], in_=ss[:, 4:8]), reads=(ss[:, 4:8],), writes=(ss[:, 8:12],))
        for b in range(4):
            hn = hn2[b % 2]
            P.stt(hn, h32[:, b, :], ss[:, 8 + b:9 + b], gbc, ALU.mult, ALU.mult)
            for cg in range(4):
                bank = self.nb()
                for ci in range(4):
                    c = cg * 4 + ci
                    P.transpose(bank[:, ci * 128:(ci + 1) * 128], hn[:, c * 128:(c + 1) * 128], self.ID32)
                self.evac(outT[:, cg * 4:cg * 4 + 4, b * 128:(b + 1) * 128],
                          bank[:, :].rearrange("p (a b) -> p a b", b=128))

    def phase1(self, l, s):
        P = self.P
        ar = self.ar
        ar.reset()
        gbc = ar.f(2048)
        P.dma(gbc, self.gbc[l:l + 1, :].partition_broadcast(128))
        h32s = [ar.f(4, 2048), ar.f(4, 2048)]
        hn2 = [ar.f(2048), ar.f(2048)]
        junk = ar.h(2048)
        sss = [ar.f(12), ar.f(12)]
        uTs = [ar.h(16, 512), ar.h(16, 512)]
        wtm = ar.h(16, NTM)
        slabs = [ar.h(16, 128) for _ in range(3)]
        stg = [ar.f(512) for _ in range(4)]
        stg8 = [ar.f(8) for _ in range(2)]
        self.pump_alloc()
        P.dma(wtm, self.WTM[l].rearrange("p (kc c) -> p kc c", c=NTM))
        srcH = self.x if l == 0 else self.H
        nst = 0

        def prep(tt):
            tok0 = s * S + tt * 512
            P.dma(h32s[tt % 2], srcH[tok0:tok0 + 512, :].rearrange("(b p) d -> p b d", p=128))
            self.norm_to_fm(h32s[tt % 2], gbc, hn2, junk, sss[tt % 2], uTs[tt % 2])
        prep(0)
        for tt in range(4):
            if tt + 1 < 4:
                prep(tt + 1)
            uT = uTs[tt % 2]
            PRE = 2
            for j in range(min(PRE, NFM)):
                P.dma(slabs[j % 3], self.WIN[l][j].rearrange("p (kc c) -> p kc c", c=128))
            for j in range(NFM):
                if j + PRE < NFM:
                    P.dma(slabs[(j + PRE) % 3], self.WIN[l][j + PRE].rearrange("p (kc c) -> p kc c", c=128))
                width = sum(n for _, n in FM[j][1])
                slab = slabs[j % 3]
                bank = self.nb()
                for kc in range(16):
                    mmop = P.mm(bank[0:width, :], slab[:, kc, 0:width], uT[:, kc, :], start=(kc == 0), stop=(kc == 15))
                if (l == 0 and s == 0) or j % 4 == 0:
                    self.pump(1, after=mmop)
                st = stg[nst % 4]
                nst += 1
                self.evac(st[0:width, :], bank[0:width, :])
                P.dma(self.PT[j * 128:j * 128 + width, tt * 512:(tt + 1) * 512], st[0:width, :])
            for b in range(4):
                bank = self.nb()
                for kc in range(16):
                    P.mm(bank[:, :], uT[:, kc, b * 128:(b + 1) * 128], wtm[:, kc, 0:512], start=(kc == 0), stop=(kc == 15))
                st = stg[nst % 4]
                nst += 1
                self.evac(st, bank[:, :])
                r0 = tt * 512 + b * 128
                P.dma(self.PNri[r0:r0 + 128, :], st)
                bank = self.nb()
                for kc in range(16):
                    P.mm(bank[:, 0:8], uT[:, kc, b * 128:(b + 1) * 128], wtm[:, kc, 512:520], start=(kc == 0), stop=(kc == 15))
                s8 = stg8[b % 2]
                self.evac(s8, bank[:, 0:8])
                P.dma(self.PNg[r0:r0 + 128, :], s8)

    def gemm_tm_add(self, h32, aT, Wc, nchunk, pieces):
        P = self.P
        nfg = nchunk // 4
        for q in range(4):
            banks = self.ps[0:4] if q % 2 == 0 else self.ps[4:8]
            for fg in range(nfg):
                pc = pieces[(q * nfg + fg) % len(pieces)]
                P.dma(pc, Wc[q * nfg + fg].rearrange("p (fi c) -> p fi c", c=512))
                for fi in range(4):
                    f = fg * 4 + fi
                    for b in range(4):
                        P.mm(banks[b][:, :], aT[:, f, b * 128:(b + 1) * 128], pc[:, fi, :],
                             start=(f == 0), stop=(f == nchunk - 1))
            for b in range(4):
                dst = h32[:, b, q * 512:(q + 1) * 512]
                P.tt("dve", dst, banks[b][:, :], dst, ALU.add)

    def phase3(self, l, s):
        P = self.P
        ar = self.ar
        ar.reset()
        gbc = ar.f(2048)
        P.dma(gbc, self.gbc[L + l:L + l + 1, :].partition_broadcast(128))
        last = (l == self.nl - 1)
        if last:
            gfin = ar.f(2048)
            P.dma(gfin, self.gbc[2 * L:2 * L + 1, :].partition_broadcast(128))
        h32 = ar.f(4, 2048)
        hn2 = [ar.f(2048), ar.f(2048)]
        junk = ar.h(2048)
        ss = ar.f(12)
        r32 = [ar.f(512) for _ in range(2)]
        vT = ar.h(16, 512)
        aT = ar.h(64, 512)
        slabs = [ar.h(16, 128) for _ in range(3)]
        pieces = [ar.h(4, 512) for _ in range(3)]
        if not last:
            self.pump_alloc()
        srcH = self.x if l == 0 else self.H
        for tt in range(4):
            tok0 = s * S + tt * 512
            P.dma(h32, srcH[tok0:tok0 + 512, :].rearrange("(b p) d -> p b d", p=128))
            P.dma(vT, self.YT[:, tt * 512:(tt + 1) * 512].rearrange("(c p) t -> p c t", p=128))
            self.gemm_tm_add(h32, vT, self.WO[l], 16, pieces)
            self.norm_to_fm(h32, gbc, hn2, junk, ss, vT)
            PRE = 2
            for f in range(PRE):
                P.dma(slabs[f % 3], self.W1[l][f].rearrange("p (kc c) -> p kc c", c=128))
            for f in range(64):
                if f + PRE < 64:
                    P.dma(slabs[(f + PRE) % 3], self.W1[l][f + PRE].rearrange("p (kc c) -> p kc c", c=128))
                slab = slabs[f % 3]
                bank = self.nb()
                for kc in range(16):
                    mmop = P.mm(bank[:, :], slab[:, kc, :], vT[:, kc, :], start=(kc == 0), stop=(kc == 15))
                if not last and f % 4 == 0:
                    self.pump(1, after=mmop)
                r = r32[f % 2]
                P.act(r, bank[:, :], AF.Relu)
                P.tt("dve", aT[:, f, :], r, r, ALU.mult)
            self.gemm_tm_add(h32, aT, self.W2[l], 64, pieces)
            if last:
                for b in range(4):
                    P.act(junk, h32[:, b, :], AF.Square, accum_out=ss[:, b:b + 1])
                P.ts("dve", ss[:, 4:8], ss[:, 0:4], 1.0 / D, EPS, ALU.mult, ALU.add)
                P.act(ss[:, 4:8], ss[:, 4:8], AF.Sqrt)
                P.add("dve", lambda e: e.reciprocal(out=ss[:, 8:12], in_=ss[:, 4:8]), reads=(ss[:, 4:8],), writes=(ss[:, 8:12],))
                for b in range(4):
                    P.stt(h32[:, b, :], h32[:, b, :], ss[:, 8 + b:9 + b], gfin, ALU.mult, ALU.mult)
                P.dma(self.out[tok0:tok0 + 512, :].rearrange("(b p) d -> p b d", p=128), h32)
            else:
                P.dma(self.H[tok0:tok0 + 512, :].rearrange("(b p) d -> p b d", p=128), h32)

    def yt_from_pt(self):
        P = self.P
        ar = self.ar
        ar.reset()
        t32 = [ar.f(2048) for _ in range(2)]
        t16 = [ar.h(2048) for _ in range(2)]
        for c in range(16):
            P.dma(t32[c % 2], self.PT[c * 128:(c + 1) * 128, :])
            P.copy("dve", t16[c % 2], t32[c % 2])
            P.dma(self.YT[c * 128:(c + 1) * 128, :], t16[c % 2])

    def build(self, mixers=True):
        self.setup()
        for l in range(self.nl):
            self.load_layer_params(l)
            self.convert_layer(l)
            for s in range(self.ns):
                self.phase1(l, s)
                if mixers:
                    self.mix_sconv(l, s)
                    self.mix_mla(l, s)
                    self.mix_gdn(l, s)
                    self.mix_hgrn(l, s)
                else:
                    self.yt_from_pt()
                self.phase3(l, s)
        self.P.emit(lambda name: self.es.enter_context(self.nc.semaphore(name)))


def host_inputs(inp, core, n_seq=SEQ_PER_CORE):
    f = lambda a: np.ascontiguousarray(np.asarray(a, np.float32))
    b0 = core * n_seq
    x = f(inp["x"][b0:b0 + n_seq]).reshape(n_seq * S, D)
    pos = np.ascontiguousarray(np.asarray(inp["positions"][b0:b0 + n_seq], np.int32))
    return {"x": x, "pos": pos}


def shared_inputs(inp):
    f = lambda a: np.ascontiguousarray(np.asarray(a, np.float32))
    gbc = np.concatenate([f(inp["norm1_g"]), f(inp["norm2_g"]), f(inp["final_g"])[None, :]], axis=0)
    lbl = np.ascontiguousarray(np.transpose(f(inp["hgrn_lb_logits"]).reshape(L, 4, 128), (2, 1, 0)).reshape(128, 16))
    return {
        "w_in": f(inp["w_in"]), "w_o": f(inp["w_o"]), "w_ff1": f(inp["w_ff1"]), "w_ff2": f(inp["w_ff2"]),
        "w_uq": f(inp["mla_w_uq"]), "w_ukv": f(inp["mla_w_ukv"]),
        "pvec": make_pvec(inp), "gbc": gbc, "lbl": lbl, "consts": make_consts(),
    }


def kernel(**inp):
    from contextlib import ExitStack
    nc = bass.Bass("TRN2", target_bir_lowering=False)
    with ExitStack() as es:
        b = Builder(nc, es)
        b.build()
    sh = shared_inputs(inp)
    in_maps = []
    for c in range(NCORES):
        m = dict(sh)
        m.update(host_inputs(inp, c))
        in_maps.append(m)
    res = run_bass_kernel_spmd(nc, in_maps, core_ids=list(range(NCORES)))
    outs = [res.results[c]["out"].reshape(SEQ_PER_CORE, S, D) for c in range(NCORES)]
    return np.concatenate(outs, axis=0).astype(np.float32)


SCALE_ATT = (128 + 64) ** -0.5
TSOLVE_BF16 = False
TWO_PI = 2.0 * math.pi
C1 = 6.28125
C2 = TWO_PI - C1


def _rstd_from(self, out, ss_ap, n, tmp):
    P = self.P
    P.act(tmp, ss_ap, AF.Sqrt, bias=self.EPSC, scale=1.0 / n)
    P.add("dve", lambda e: e.reciprocal(out=out, in_=tmp), reads=(tmp,), writes=(out,))


def _pt_rows(self, name, c=0, rows=128):
    j = FMI[name] + c
    return self.PT[j * 128:j * 128 + rows, :]


def setup_rope(self):
    P = self.P
    ar = self.ar
    PI = math.pi
    for s in range(self.ns):
        ar.reset()
        posi = ar.f(2048)[0:64, :].bitcast(I32)
        P.dma(posi, self.pos[s:s + 1, :].partition_broadcast(64))
        ang = ar.f(2048)[0:64, :]
        P.copy("dve", ang, posi)
        P.ts("dve", ang, ang, self.c32("INVF", 1, 64), None, ALU.mult)
        kf = ar.f(2048)[0:64, :]
        ki = ar.f(2048)[0:64, :].bitcast(I32)
        P.ts("dve", kf, ang, 1.0 / TWO_PI, None, ALU.mult)
        P.copy("dve", ki, kf)
        P.copy("dve", kf, ki)
        r = ar.f(2048)[0:64, :]
        P.stt(r, kf, -C1, ang, ALU.mult, ALU.add)
        P.stt(r, kf, -C2, r, ALU.mult, ALU.add)
        m = ar.f(2048)[0:64, :]

        def fold(v):
            P.ts("dve", m, v, PI, -TWO_PI, ALU.is_gt, ALU.mult)
            P.tt("dve", v, v, m, ALU.add)
            P.ts("dve", m, v, -PI, TWO_PI, ALU.is_lt, ALU.mult)
            P.tt("dve", v, v, m, ALU.add)
            P.ts("dve", v, v, -3.14159, 3.14159, ALU.max, ALU.min)
        fold(r)
        fold(r)
        sn = ar.f(2048)[0:64, :]
        P.act(sn, r, AF.Sin, scale=self.c32("SIGN", 1, 64))
        P.dma(self.ROPE[(s * 2 + 1) * 64:(s * 2 + 2) * 64, :], sn)
        rc = ar.f(2048)[0:64, :]
        P.ts("dve", rc, r, PI / 2, None, ALU.add)
        fold(rc)
        cs = ar.f(2048)[0:64, :]
        P.act(cs, rc, AF.Sin)
        P.dma(self.ROPE[(s * 2) * 64:(s * 2 + 1) * 64, :], cs)


def mix_sconv(self, l, s):
    P = self.P
    ar = self.ar
    ar.reset()
    y = ar.f(4, 2048)
    sq = ar.h(4, 2048)
    zp = [ar.f(2050) for _ in range(2)]
    xin = [ar.f(2048) for _ in range(2)]
    cin = [ar.f(2048) for _ in range(2)]
    bin_ = [ar.f(2048) for _ in range(2)]
    rstd = ar.f(2048)
    tmp = ar.f(512)
    yo = [ar.h(2048) for _ in range(2)]
    for c in range(4):
        x_, c_, b_, z_ = xin[c % 2], cin[c % 2], bin_[c % 2], zp[c % 2]
        P.dma(x_, _pt_rows(self, "sc_x", c))
        P.dma(c_, _pt_rows(self, "sc_c", c))
        P.dma(b_, _pt_rows(self, "sc_b", c))
        P.memset("pool", z_[:, 0:2], 0.0)
        P.tt("pool", z_[:, 2:2050], x_, c_, ALU.mult)
        acc = y[:, c, :]
        P.act(acc, z_[:, 2:2050], AF.Copy, scale=self.pv("scw", 2 * 4 + c))
        P.stt(acc, z_[:, 1:2049], self.pv("scw", 1 * 4 + c), acc, ALU.mult, ALU.add)
        P.stt(acc, z_[:, 0:2048], self.pv("scw", 0 * 4 + c), acc, ALU.mult, ALU.add)
        P.tt("pool", acc, acc, b_, ALU.mult)
        P.act(sq[:, c, :], acc, AF.Square)
    for tt in range(4):
        bank = self.nb()
        sl = slice(tt * 512, (tt + 1) * 512)
        for c in range(4):
            P.mm(bank[:, :], self.ONES16, sq[:, c, sl], start=(c == 0), stop=(c == 3))
        _rstd_from(self, rstd[:, sl], bank[:, :], 512, tmp)
    for c in range(4):
        o = yo[c % 2]
        P.stt(o, y[:, c, :], self.pv("scg", c), rstd, ALU.mult, ALU.mult)
        P.dma(self.YT[c * 128:(c + 1) * 128, :], o)


def mix_mla(self, l, s):
    P = self.P
    ar = self.ar
    ar.reset()
    wq = ar.h(3, 768)
    wqs = ar.h(3, 256)
    wkv = ar.h(2, 1024)
    wv = ar.h(2, 512)
    mark = ar.off
    w32q = ar.f(3, 768)
    w32kv = ar.f(2, 1024)
    P.dma(w32q, self.w_uq[l].rearrange("(kc p) c -> p kc c", p=128))
    P.dma(w32kv, self.w_ukv[l].rearrange("(kc p) c -> p kc c", p=128))
    P.copy("dve", wq, w32q)
    P.copy("pool", wkv, w32kv)
    for kc in range(3):
        src = w32q[:, kc, :].rearrange("p (h d) -> p h d", d=192)
        dst = wqs[:, kc, :].rearrange("p (h d) -> p h d", d=64)
        P.copy("dve", dst[:, :, 0:32], src[:, :, 160:192])
        P.copy("dve", dst[:, :, 32:64], src[:, :, 128:160])
    for kc in range(2):
        src = w32kv[:, kc, :].rearrange("p (h d) -> p h d", d=256)
        P.copy("pool", wv[:, kc, :].rearrange("p (h d) -> p h d", d=128), src[:, :, 128:256])
    ar.reset(mark)
    qn = ar.h(4, 2048)
    qr = ar.h(4, 2048)
    kn = ar.h(4, 2048)
    kro = ar.h(2048)
    vtm = ar.h(16, 512)
    cos2 = ar.f(2048)
    sin2 = ar.f(2048)
    P.dma(cos2[0:64, :], self.ROPE[(s * 2) * 64:(s * 2 + 1) * 64, :])
    P.dma(sin2[0:64, :], self.ROPE[(s * 2 + 1) * 64:(s * 2 + 2) * 64, :])
    mark = ar.off
    cqn = ar.h(3, 2048)
    ckvn = ar.h(2, 2048)
    mark2 = ar.off
    cq = ar.f(3, 2048)
    sq = ar.h(3, 2048)
    rstd = ar.f(512)
    tmp = ar.f(512)
    for (name, nch, gname, dst) in (("m_cq", 3, "mqg", cqn), ("m_ckv", 2, "mkvg", ckvn)):
        for c in range(nch):
            P.dma(cq[:, c, :], _pt_rows(self, name, c))
            P.act(sq[:, c, :], cq[:, c, :], AF.Square)
        for tt in range(4):
            sl = slice(tt * 512, (tt + 1) * 512)
            bank = self.nb()
            for c in range(nch):
                P.mm(bank[:, :], self.ONES16, sq[:, c, sl], start=(c == 0), stop=(c == nch - 1))
            _rstd_from(self, rstd, bank[:, :], nch * 128, tmp)
            for c in range(nch):
                P.stt(dst[:, c, sl], cq[:, c, sl], self.pv(gname, c), rstd, ALU.mult, ALU.mult)
    ar.reset(mark2)
    kr = ar.f(2048)
    krs = ar.f(2048)
    P.dma(kr[0:64, :], _pt_rows(self, "m_kr", 0, 64))
    P.dma(krs[0:64, :], _pt_rows(self, "m_krs", 0, 64))
    P.tt("dve", kr[0:64, :], kr[0:64, :], cos2[0:64, :], ALU.mult)
    P.tt("pool", krs[0:64, :], krs[0:64, :], sin2[0:64, :], ALU.mult)
    P.tt("dve", kro[0:64, :], kr[0:64, :], krs[0:64, :], ALU.add)
    t1 = ar.f(512)
    t2 = ar.f(512)
    for tt in range(4):
        sl = slice(tt * 512, (tt + 1) * 512)
        for h in range(4):
            bank = self.nb()
            for kc in range(3):
                P.mm(bank[:, :], wq[:, kc, h * 192:h * 192 + 128], cqn[:, kc, sl], start=(kc == 0), stop=(kc == 2))
            self.evac(qn[:, h, sl], bank[:, :])
            bx = self.nb()
            for kc in range(3):
                P.mm(bx[0:64, :], wq[:, kc, h * 192 + 128:h * 192 + 192], cqn[:, kc, sl], start=(kc == 0), stop=(kc == 2))
            bxs = self.nb()
            for kc in range(3):
                P.mm(bxs[0:64, :], wqs[:, kc, h * 64:(h + 1) * 64], cqn[:, kc, sl], start=(kc == 0), stop=(kc == 2))
            P.tt("dve", t1[0:64, :], bx[0:64, :], cos2[0:64, sl], ALU.mult)
            P.tt("dve", t2[0:64, :], bxs[0:64, :], sin2[0:64, sl], ALU.mult)
            P.tt("pool", qr[0:64, h, sl], t1[0:64, :], t2[0:64, :], ALU.add)
            bank = self.nb()
            for kc in range(2):
                P.mm(bank[:, :], wkv[:, kc, h * 256:h * 256 + 128], ckvn[:, kc, sl], start=(kc == 0), stop=(kc == 1))
            self.evac(kn[:, h, sl], bank[:, :])
    for blk in range(16):
        bank = self.nb()
        for kc in range(2):
            P.mm(bank[:, :], ckvn[:, kc, blk * 128:(blk + 1) * 128], wv[:, kc, :], start=(kc == 0), stop=(kc == 1))
        self.evac(vtm[:, blk, :], bank[:, :])
    ar.reset(mark)
    pbuf = [ar.h(512) for _ in range(3)]
    o32 = ar.f(4, 512)
    rl = ar.f(512)
    sqo = ar.h(4, 512)
    rstd = ar.f(512)
    tmp = ar.f(512)
    ybuf = [ar.h(4, 512) for _ in range(2)]
    sb_i = 0
    pb_i = 0
    acc_i = 0
    for tt in range(4):
        q0 = tt * 512
        for h in range(4):
            Ob = self.ps[4 + acc_i % 2]
            Lb = self.ps[6 + acc_i % 2]
            acc_i += 1
            nkb = 4 * tt + 4
            Sbs = {}

            def issue_S(kb):
                nonlocal sb_i
                j = kb - 4 * tt
                c0 = 128 * j if j > 0 else 0
                Sb = self.ps[sb_i % 4]
                sb_i += 1
                ks = slice(kb * 128, (kb + 1) * 128)
                P.mm(Sb[:, c0:512], kn[:, h, ks], qn[:, h, q0 + c0:q0 + 512], start=True, stop=False)
                P.mm(Sb[:, c0:512], kro[0:64, ks], qr[0:64, h, q0 + c0:q0 + 512], start=False, stop=True)
                Sbs[kb] = (Sb, c0, j)
            issue_S(0)
            for kb in range(nkb):
                if kb + 1 < nkb:
                    issue_S(kb + 1)
                Sb, c0, j = Sbs.pop(kb)
                pT = pbuf[pb_i % 3]
                pb_i += 1
                P.act(pT[:, c0:512], Sb[:, c0:512], AF.Exp, scale=SCALE_ATT)
                if j >= 0:
                    P.tt("pool", pT[:, c0:c0 + 128], pT[:, c0:c0 + 128], self.CAUS16, ALU.mult)
                P.mm(Ob[:, c0:512], vtm[:, kb, h * 128:(h + 1) * 128], pT[:, c0:512], start=(kb == 0), stop=(kb == nkb - 1))
                P.mm(Lb[:, c0:512], self.ONES16, pT[:, c0:512], start=(kb == 0), stop=(kb == nkb - 1))
            self.pump(1, after=P.ops["pe"][-1])
            P.add("dve", lambda e, Lb=Lb: e.reciprocal(out=rl, in_=Lb[:, :]), reads=(Lb[:, :],), writes=(rl,))
            P.tt("dve", o32[:, h, :], Ob[:, :], rl, ALU.mult)
            P.act(sqo[:, h, :], o32[:, h, :], AF.Square)
        bank = self.ps[sb_i % 4]
        sb_i += 1
        for h in range(4):
            P.mm(bank[:, :], self.ONES16, sqo[:, h, :], start=(h == 0), stop=(h == 3))
        _rstd_from(self, rstd, bank[:, :], 512, tmp)
        yb = ybuf[tt % 2]
        for h in range(4):
            P.stt(yb[:, h, :], o32[:, h, :], self.pv("mog", h), rstd, ALU.mult, ALU.mult)
        P.dma(self.YT[512:1024, q0:q0 + 512].rearrange("(h p) t -> p h t", p=128), yb)


Builder.setup_rope = setup_rope
Builder.mix_sconv = mix_sconv
Builder.mix_mla = mix_mla


def build2(self, mixers=("sconv", "mla", "gdn", "hgrn")):
    self.setup()
    self.setup_rope()
    items0 = self.convert_items(0)
    self.convert_now(items0[:NFM + 5])
    self.convq = items0[NFM + 5:]
    for l in range(self.nl):
        self.load_layer_params(l)
        if self.convq and l > 0:
            self.convert_now(self.convq)
            self.convq = []
        for s in range(self.ns):
            self.phase1(l, s)
            if s == 0 and l + 1 < self.nl:
                if self.convq:
                    self.convert_now(self.convq)
                self.convq = self.convert_items(l + 1)
            if mixers is None:
                self.yt_from_pt()
            else:
                for m in mixers:
                    getattr(self, "mix_" + m)(l, s)
            self.phase3(l, s)
    self.P.emit(lambda name: self.es.enter_context(self.nc.semaphore(name)))


Builder.build = build2


def mix_hgrn(self, l, s):
    P = self.P
    ar = self.ar
    ar.reset()
    o32 = ar.h(4, 2048)
    mark0 = ar.off
    qeT = ar.h(4, 2048)
    keT = ar.h(4, 2048)
    qsT = ar.h(4, 2048)
    kdtm = ar.h(4, 32 * 128)
    itm = ar.h(32, 512)
    ebl = ar.f(4, 32)
    S32 = ar.f(4, 128)
    S16 = ar.h(4, 128)
    mark = ar.off
    stg = [ar.f(4, 512) for _ in range(2)]
    src = self.PNri.rearrange("(n p) f -> p n f", p=64)
    for g in range(8):
        st = stg[g % 2]
        P.dma(st[0:64], src[:, g * 4:(g + 1) * 4, :])
        P.copy(self.rr(("act", "pool")), itm[0:64, g * 4:(g + 1) * 4, :], st[0:64])
    ar.reset(mark)
    t_qs = [ar.f(2048), ar.f(2048)]
    t_ks = [ar.f(2048), ar.f(2048)]
    t_d = ar.f(2048)
    t_bc = ar.f(2048)
    t_e = ar.f(2048)
    def v3(t):
        return t.rearrange("p (n c) -> p n c", c=64)
    for h in range(4):
        t_q, t_k = t_qs[h % 2], t_ks[h % 2]
        lbc = self.LB[:, h * 4 + l:h * 4 + l + 1]
        oml = self.LB[:, 16 + h * 4 + l:16 + h * 4 + l + 1]
        noml = self.LB[:, 32 + h * 4 + l:32 + h * 4 + l + 1]
        P.dma(t_q, _pt_rows(self, "r_q", h))
        P.dma(t_k, _pt_rows(self, "r_f", h))
        P.act(t_q, t_q, AF.Silu)
        P.act(t_k, t_k, AF.Sigmoid)
        P.act(t_d, t_k, AF.Ln, bias=lbc, scale=oml)
        P.ts("pool", t_k, t_k, noml, oml, ALU.mult, ALU.add)
        P.add("dve", lambda e: e.tensor_tensor_scan(out=t_bc, data0=self.SCANM16, data1=t_d, initial=0.0,
                                                    op0=ALU.mult, op1=ALU.add),
              reads=(self.SCANM16, t_d), writes=(t_bc,))
        bc3 = v3(t_bc)
        P.act(ebl[:, h, :], bc3[:, :, 63], AF.Exp)
        P.tt("dve", v3(t_d), bc3, bc3[:, :, 31:32].broadcast_to([128, 32, 64]), ALU.subtract)
        P.act(t_e, t_d, AF.Exp)
        P.tt("dve", qeT[:, h, :], t_q, t_e, ALU.mult)
        P.act(t_e, t_d, AF.Exp, scale=-1.0)
        P.tt("pool", keT[:, h, :], t_k, t_e, ALU.mult)
        P.act(t_e, t_bc, AF.Exp)
        P.tt("dve", qsT[:, h, :], t_q, t_e, ALU.mult)
        P.tt("dve", v3(t_d), bc3[:, :, 63:64].broadcast_to([128, 32, 64]), bc3, ALU.subtract)
        P.act(t_e, t_d, AF.Exp)
        P.tt("pool", t_d, t_k, t_e, ALU.mult)
        for g in range(8):
            bank = self.ps[g % 4]
            for i in range(4):
                n = g * 4 + i
                P.transpose(bank[0:64, i * 128:(i + 1) * 128], t_d[:, n * 64:(n + 1) * 64], self.ID32)
            self.evac(kdtm[0:64, h, g * 512:(g + 1) * 512], bank[0:64, :])
        self.pump(2, after=P.ops["pe"][-1])
    ar.reset(mark)
    attnT = ar.h(16, 512)
    MU8 = self.c32("MU8", 512, 64)
    for blk in range(16):
        bank = self.ps[blk % 4]
        for ci in range(2):
            n = blk * 2 + ci
            cs = slice(n * 64, (n + 1) * 64)
            for h in range(4):
                col = (ci * 4 + h) * 64
                P.mm(bank[0:64, col:col + 64], keT[:, h, cs], qeT[:, h, cs], start=True, stop=True)
        P.tt("dve", attnT[0:64, blk, :], bank[0:64, :], MU8, ALU.mult)
    P.memset("dve", S32, 0.0)
    P.memset("pool", S16, 0.0)
    for blk in range(16):
        Ob = self.ps[4 + blk % 2]
        for ci in range(2):
            n = blk * 2 + ci
            cs = slice(n * 64, (n + 1) * 64)
            Sb = self.ps[6 + n % 2]
            for h in range(4):
                ocol = (h * 2 + ci) * 64
                acol = (ci * 4 + h) * 64
                iv = itm[0:64, n, h * 128:(h + 1) * 128]
                P.mm(Ob[:, ocol:ocol + 64], S16[:, h, :], qsT[:, h, cs], start=True, stop=False)
                P.mm(Ob[:, ocol:ocol + 64], iv, attnT[0:64, blk, acol:acol + 64], start=False, stop=True)
                P.mm(Sb[:, h * 128:(h + 1) * 128], kdtm[0:64, h, n * 128:(n + 1) * 128], iv, start=True, stop=True)
            for h in range(4):
                P.stt(S32[:, h, :], S32[:, h, :], ebl[:, h, n:n + 1], Sb[:, h * 128:(h + 1) * 128], ALU.mult, ALU.add)
            P.copy("act", S16, S32)
        P.copy("pool" if False else "act", o32[:, :, blk * 128:(blk + 1) * 128],
               Ob[:, :].rearrange("p (h t) -> p h t", t=128))
    ar.reset(mark0)
    _out_norm_gate(self, o32, "hng", "r_z", AF.Sigmoid, 1536)


Builder.mix_hgrn = mix_hgrn


def _out_norm_gate(self, o, gname, zname, gate_func, row0):
    P = self.P
    ar = self.ar
    sqs = [ar.h(2048) for _ in range(2)]
    rstds = [ar.f(2048) for _ in range(2)]
    tmps = [ar.f(512) for _ in range(2)]
    rzs = [ar.f(2048) for _ in range(2)]
    t_ys = [ar.f(2048) for _ in range(2)]
    yo = [ar.h(2048) for _ in range(2)]
    for h in range(4):
        rz, sq, rstd, t_y, tmp = rzs[h % 2], sqs[h % 2], rstds[h % 2], t_ys[h % 2], tmps[h % 2]
        P.dma(rz, _pt_rows(self, zname, h))
        P.act(rz, rz, gate_func)
        P.act(sq, o[:, h, :], AF.Square)
        for tt in range(4):
            sl = slice(tt * 512, (tt + 1) * 512)
            bank = self.ps[(h * 4 + tt) % 8]
            P.mm(bank[:, :], self.ONES16, sq[:, sl], start=True, stop=True)
            _rstd_from(self, rstd[:, sl], bank[:, :], 128, tmp)
        P.stt(t_y, o[:, h, :], self.pv(gname), rstd, ALU.mult, ALU.mult)
        P.tt("pool", yo[h % 2], t_y, rz, ALU.mult)
        P.dma(self.YT[row0 + h * 128:row0 + (h + 1) * 128, :], yo[h % 2])


def mix_gdn(self, l, s):
    P = self.P
    ar = self.ar
    ar.reset()
    o16 = ar.h(4, 2048)
    mark0 = ar.off
    kT16 = ar.h(4, 2048)
    kd = ar.h(4, 32 * 128)
    bV = ar.h(4, 32 * 128)
    gam = ar.f(32, 4)
    beta = ar.f(32, 4)
    nbeg = ar.f(32, 4)
    dkl = ar.f(32, 4)
    egl = ar.f(32, 4)
    gtm = ar.f(32, 4)
    S32 = ar.f(4, 128)
    S16 = ar.h(4, 128)
    qT16 = ar.h(4, 2048)
    mark2 = ar.off
    gab = ar.f(32, 8)
    P.dma(gab[0:64], self.PNg.rearrange("(n p) c -> p n c", p=64))
    xs = ar.f(32, 4)
    ax = ar.f(32, 4)
    Aex = ar.f(4)
    dtb_b = self.pv("dtb", 0, 4)[0:64].rearrange("p (o c) -> p o c", o=1).broadcast_to([64, 32, 4])
    P.act(beta[0:64], gab[0:64, :, 4:8], AF.Sigmoid)
    P.tt("dve", xs[0:64], gab[0:64, :, 0:4], dtb_b, ALU.add)
    P.act(ax[0:64], xs[0:64], AF.Abs)
    P.act(ax[0:64], ax[0:64], AF.Exp, scale=-1.0)
    P.act(ax[0:64], ax[0:64], AF.Ln, bias=1.0)
    P.ts("dve", xs[0:64], xs[0:64], 0.0, None, ALU.max)
    P.tt("dve", xs[0:64], xs[0:64], ax[0:64], ALU.add)
    P.act(Aex[0:64], self.pv("alog", 0, 4)[0:64], AF.Exp)
    P.ts("dve", Aex[0:64], Aex[0:64], -1.0, None, ALU.mult)
    P.tt("dve", gtm[0:64], xs[0:64], Aex[0:64].rearrange("p (o c) -> p o c", o=1).broadcast_to([64, 32, 4]), ALU.mult)
    g2 = gtm[0:64].rearrange("p n h -> p (n h)")
    b0 = self.ps[0]
    P.mm(b0[0:64, 0:128], self.c32("TRI", 64, 64), g2, start=True, stop=True)
    P.copy("dve", gam[0:64].rearrange("p n h -> p (n h)"), b0[0:64, 0:128])
    b1 = self.ps[1]
    P.mm(b1[:, 0:128], self.C32[0:64, CO["ONES"]:CO["ONES"] + 128], g2, start=True, stop=True)
    P.act(egl.rearrange("p n h -> p (n h)"), b1[:, 0:128], AF.Exp)
    P.tt("dve", dkl[0:64].rearrange("p n h -> p (n h)"), b1[0:64, 0:128], gam[0:64].rearrange("p n h -> p (n h)"), ALU.subtract)
    P.act(dkl[0:64], dkl[0:64], AF.Exp)
    P.act(nbeg[0:64], gam[0:64], AF.Exp)
    P.tt("dve", nbeg[0:64], nbeg[0:64], beta[0:64], ALU.mult)
    P.ts("dve", nbeg[0:64], nbeg[0:64], -1.0, None, ALU.mult)
    xpad = [ar.f(2051) for _ in range(2)]
    acc = [ar.f(2048) for _ in range(2)]
    sq = ar.h(2048)
    rstd = ar.f(2048)
    tmp = ar.f(512)
    for kind, base in (("k", 4), ("v", 8), ("q", 0)):
        for h in range(4):
            cc = base + h
            xp = xpad[cc % 2]
            a = acc[cc % 2]
            P.memset("pool", xp[:, 0:3], 0.0)
            P.dma(xp[:, 3:2051], _pt_rows(self, {"q": "g_q", "k": "g_k", "v": "g_v"}[kind], h))
            P.act(a, xp[:, 3:2051], AF.Copy, scale=self.pv("gcw", 3 * 12 + cc))
            for j in (2, 1, 0):
                P.stt(a, xp[:, j:j + 2048], self.pv("gcw", j * 12 + cc), a, ALU.mult, ALU.add)
            P.act(a, a, AF.Silu)
            if kind != "v":
                P.act(sq, a, AF.Square)
                for tt in range(4):
                    sl = slice(tt * 512, (tt + 1) * 512)
                    bank = self.ps[tt % 4]
                    P.mm(bank[:, :], self.ONES16, sq[:, sl], start=True, stop=True)
                    _rstd_from(self, rstd[:, sl], bank[:, :], 1, tmp)
            if kind == "q":
                P.stt(qT16[:, h, :], a, 128.0 ** -0.5, rstd, ALU.mult, ALU.mult)
                continue
            if kind == "k":
                P.tt("pool", a, a, rstd, ALU.mult)
                P.copy("act", kT16[:, h, :], a)
            dst = kd if kind == "k" else bV
            sc = dkl if kind == "k" else beta
            for g in range(8):
                bank = self.ps[4 + g % 4]
                for i in range(4):
                    n = g * 4 + i
                    P.transpose(bank[0:64, i * 128:(i + 1) * 128], a[:, n * 64:(n + 1) * 64], self.ID32)
                P.tt("dve", dst[0:64, h, g * 512:(g + 1) * 512].rearrange("p (n d) -> p n d", d=128),
                     bank[0:64, :].rearrange("p (n d) -> p n d", d=128),
                     sc[0:64, g * 4:(g + 1) * 4, h:h + 1].broadcast_to([64, 4, 128]), ALU.mult)
    ar.reset(mark2)
    Tt16 = ar.h(16, 512)
    attnT = ar.h(16, 512)
    qe16 = ar.h(4, 2048)
    mark1 = ar.off
    eG = ar.f(512)
    tD = ar.f(512)
    decT = ar.f(512)
    M0f = ar.f(512)
    if TSOLVE_BF16:
        Mm = [ar.h(512), ar.h(512)]
        Nn = [ar.h(512), ar.h(512)]
        Tt = ar.h(512)
    else:
        Mm = [M0f, ar.f(512)]
        Nn = [ar.f(512), ar.f(512)]
        Tt = ar.f(512)
    bs = ar.f(512)
    NEGM8 = self.c32("NEGM8", 512, 64)
    NEGS8 = self.c32("NEGS8", 512, 64)
    I8 = self.c32("I8", 512, 64)
    TRI = self.c32("TRI", 64, 64)
    ID64 = self.C32[0:64, CO["ID"]:CO["ID"] + 64]
    def v8(t):
        return t[0:64].rearrange("p (g t) -> p g t", t=64)
    for blk in range(16):
        bG, bK, bQ, bB = self.ps[0], self.ps[1], self.ps[2], self.ps[3]
        for ci in range(2):
            n = blk * 2 + ci
            cs = slice(n * 64, (n + 1) * 64)
            for h in range(4):
                col = (ci * 4 + h) * 64
                P.mm(bG[:, col:col + 64], gtm[0:64, n, h:h + 1].broadcast_to([64, 128]), TRI, start=True, stop=True)
                P.mm(bK[0:64, col:col + 64], kT16[:, h, cs], kT16[:, h, cs], start=True, stop=True)
                P.mm(bQ[0:64, col:col + 64], kT16[:, h, cs], qT16[:, h, cs], start=True, stop=True)
                P.mm(bB[0:64, col:col + 64], beta[0:64, n, h:h + 1].broadcast_to([64, 64]), ID64, start=True, stop=True)
        P.act(eG, bG[:, :], AF.Exp)
        gcol = gam[0:64, 2 * blk:2 * blk + 2, :].rearrange("p n h -> p (n h)").rearrange("p (g o) -> p g o", o=1).broadcast_to([64, 8, 64])
        P.tt("dve", v8(tD), v8(bG), gcol, ALU.subtract)
        P.tt("pool", tD[0:64], tD[0:64], NEGM8, ALU.add)
        P.act(decT[0:64], tD[0:64], AF.Exp)
        for ci in range(2):
            n = blk * 2 + ci
            cs = slice(n * 64, (n + 1) * 64)
            P.tt("pool" if ci else "dve", qe16[:, :, cs], qT16[:, :, cs],
                 eG[:, ci * 256:(ci + 1) * 256].rearrange("p (h t) -> p h t", t=64), ALU.mult)
        P.tt("dve", attnT[0:64, blk, :], bQ[0:64, :], decT[0:64], ALU.mult)
        M0, N0 = Mm[0], Nn[0]
        P.tt("dve", M0f[0:64], bK[0:64, :], decT[0:64], ALU.mult)
        P.tt("dve", bs[0:64], bB[0:64, :], NEGS8, ALU.mult)
        P.tt("pool", M0f[0:64], M0f[0:64], bs[0:64], ALU.mult)
        bN = self.ps[4]
        for g in range(8):
            P.transpose(bN[0:64, g * 64:(g + 1) * 64], M0f[0:64, g * 64:(g + 1) * 64], ID64)
        P.copy("act", N0[0:64], bN[0:64, :])
        if TSOLVE_BF16:
            P.copy("pool", M0[0:64], M0f[0:64])
        P.tt("pool", Tt[0:64], M0f[0:64], I8, ALU.add)
        bMh = (self.ps[5], self.ps[0])
        bNh = (self.ps[6], self.ps[1])
        bTh = (self.ps[7], self.ps[2])
        for lev in range(1, 6):
            Mp, Np = Mm[(lev - 1) % 2], Nn[(lev - 1) % 2]
            Mq, Nq = Mm[lev % 2], Nn[lev % 2]
            for half in range(2):
                hs = slice(half * 256, (half + 1) * 256)
                bM, bN2 = bMh[half], bNh[half]
                if lev < 5:
                    for g in range(half * 4, half * 4 + 4):
                        c = slice(g * 64, (g + 1) * 64)
                        P.mm(bM[0:64, c], Np[0:64, c], Mp[0:64, c], start=True, stop=True)
                for g in range(half * 4, half * 4 + 4):
                    c = slice(g * 64, (g + 1) * 64)
                    P.mm(bN2[0:64, c], Mp[0:64, c], Np[0:64, c], start=True, stop=True)
                if lev < 5:
                    P.copy("act", Mq[0:64, hs], bM[0:64, hs])
                P.copy("dve", Nq[0:64, hs], bN2[0:64, hs])
            for half in range(2):
                hs = slice(half * 256, (half + 1) * 256)
                bT = bTh[half]
                for g in range(half * 4, half * 4 + 4):
                    c = slice(g * 64, (g + 1) * 64)
                    P.mm(bT[0:64, c], Nq[0:64, c], Tt[0:64, c], start=True, stop=True)
                P.tt("dve", Tt[0:64, hs], bT[0:64, hs], Tt[0:64, hs], ALU.add)
        P.copy("act", Tt16[0:64, blk, :], Tt[0:64])
        self.pump(3, after=P.ops["pe"][-1])
    ar.reset(mark1)
    R16 = [ar.h(512), ar.h(512)]
    Vn16 = [ar.h(512), ar.h(512)]
    P.memset("dve", S32, 0.0)
    P.memset("pool", S16, 0.0)
    for blk in range(16):
        Ob = self.ps[4 + blk % 2]
        for ci in range(2):
            n = blk * 2 + ci
            cs = slice(n * 64, (n + 1) * 64)
            bKS = self.ps[n % 2]
            bV_ = self.ps[2 + n % 2]
            Sb = self.ps[6 + n % 2]
            R = R16[n % 2]
            Vn = Vn16[n % 2]
            for h in range(4):
                P.mm(bKS[0:64, h * 128:(h + 1) * 128], kT16[:, h, cs], S16[:, h, :], start=True, stop=True)
            for h in range(4):
                P.stt(R[0:64, h * 128:(h + 1) * 128], bKS[0:64, h * 128:(h + 1) * 128], nbeg[0:64, n, h:h + 1],
                      bV[0:64, h, n * 128:(n + 1) * 128], ALU.mult, ALU.add)
            for h in range(4):
                tcol = (ci * 4 + h) * 64
                P.mm(bV_[0:64, h * 128:(h + 1) * 128], Tt16[0:64, blk, tcol:tcol + 64], R[0:64, h * 128:(h + 1) * 128],
                     start=True, stop=True)
            P.copy("act", Vn[0:64], bV_[0:64, :])
            for h in range(4):
                ocol = (h * 2 + ci) * 64
                acol = (ci * 4 + h) * 64
                P.mm(Ob[:, ocol:ocol + 64], S16[:, h, :], qe16[:, h, cs], start=True, stop=False)
                P.mm(Ob[:, ocol:ocol + 64], Vn[0:64, h * 128:(h + 1) * 128], attnT[0:64, blk, acol:acol + 64],
                     start=False, stop=True)
                P.mm(Sb[:, h * 128:(h + 1) * 128], kd[0:64, h, n * 128:(n + 1) * 128], Vn[0:64, h * 128:(h + 1) * 128],
                     start=True, stop=True)
            for h in range(4):
                P.stt(S32[:, h, :], S32[:, h, :], egl[:, n, h:h + 1], Sb[:, h * 128:(h + 1) * 128], ALU.mult, ALU.add)
            P.copy("act", S16, S32)
        P.copy("act", o16[:, :, blk * 128:(blk + 1) * 128], Ob[:, :].rearrange("p (h t) -> p h t", t=128))
    ar.reset(mark0)
    _out_norm_gate(self, o16, "gng", "g_z", AF.Silu, 1024)


Builder.mix_gdn = mix_gdn
```

```python
import math
import numpy as np
import concourse.bass as bass
import concourse.mybir as mybir
from concourse.bass_utils import run_bass_kernel_spmd

F32 = mybir.dt.float32
BF16 = mybir.dt.bfloat16
I32 = mybir.dt.int32
AF = mybir.ActivationFunctionType
ALU = mybir.AluOpType
AX = mybir.AxisListType

D = 2048
S = 2048
L = 4
DIN = 6344
DFF = 8192
NCORES = 8
SEQ_PER_CORE = 2
EPS = 1e-6

_ESZ = {F32: 4, BF16: 2, I32: 4}


def _prod(xs):
    r = 1
    for x in xs:
        r *= int(x)
    return r


class Op:
    __slots__ = ("eng", "fn", "deps", "sig", "sem", "val", "is_dma", "prev_slot")

    def __init__(self, eng, fn):
        self.eng = eng
        self.fn = fn
        self.deps = set()
        self.sig = False
        self.sem = None
        self.val = 0
        self.is_dma = False
        self.prev_slot = None


class Prog:
    ENGS = ("pe", "act", "dve", "pool", "sp")
    RING = {"sp": 40, "act": 16, "pool": 16}

    def __init__(self, nc):
        self.nc = nc
        self.ops = {e: [] for e in self.ENGS}
        self.track = {}
        self.untracked = set()
        self.dma_n = {q: 0 for q in self.RING}
        self.ring_last = {q: [None] * n for q, n in self.RING.items()}
        self.all_dmas = []
        self.extra_dep = None

    def region(self, ap):
        name = ap.name
        if name in self.untracked:
            return None
        t = ap.tensor
        shape = tuple(t.shape)
        esz = _ESZ[ap.dtype]
        rowlen = _prod(shape[1:])
        off = int(ap.offset)
        pat = ap.ap
        r0 = off // rowlen
        c0 = off % rowlen
        if type(t).__name__.startswith("PSum"):
            return (name, 0, 128, 0, 2048)
        if type(t).__name__.startswith("DRam"):
            rext = 0
            cext = 0
            for (st, cnt) in pat:
                if cnt <= 1:
                    continue
                if st % rowlen == 0:
                    rext += (st // rowlen) * (cnt - 1)
                else:
                    cext += st * (cnt - 1)
            if c0 + cext + 1 > rowlen:
                rext += (c0 + cext) // rowlen
                c0, cext = 0, rowlen - 1
            return (name, r0, r0 + rext + 1, c0 * esz, (c0 + cext + 1) * esz)
        pcnt = pat[0][1]
        fext = 0
        for (st, cnt) in pat[1:]:
            if cnt > 1:
                fext += st * (cnt - 1)
        return (name, r0, r0 + pcnt, c0 * esz, (c0 + fext + 1) * esz)

    def _deps_and_record(self, op, ap, is_write):
        reg = self.region(ap)
        if reg is None:
            return
        name, p0, p1, b0, b1 = reg
        psum = type(ap.tensor).__name__.startswith("PSum")
        recs = self.track.setdefault(name, [])
        keep = []
        for rec in recs:
            q0, q1, c0, c1, kind, rop = rec
            ov = (p0 < q1 and q0 < p1 and b0 < c1 and c0 < b1)
            if ov and (is_write or kind == "W" or (psum and rop.eng != op.eng)):
                if rop is not op:
                    op.deps.add(rop)
            if is_write and ov and q0 >= p0 and q1 <= p1 and c0 >= b0 and c1 <= b1:
                continue
            if (not is_write) and kind == "R" and rop.eng == op.eng and (q0, q1, c0, c1) == (p0, p1, b0, b1) \
                    and not rop.is_dma and not op.is_dma:
                continue
            keep.append(rec)
        keep.append((p0, p1, b0, b1, "W" if is_write else "R", op))
        self.track[name] = keep

    def add(self, eng, fn, reads=(), writes=()):
        op = Op(eng, fn)
        for ap in reads:
            if ap is not None and not isinstance(ap, (int, float)):
                self._deps_and_record(op, ap, False)
        for ap in writes:
            if ap is not None:
                self._deps_and_record(op, ap, True)
        if eng == "pe":
            op.deps = {d for d in op.deps if d.eng != "pe" or d.is_dma}
        self.ops[eng].append(op)
        return op

    def dma(self, out, in_, q="sp"):
        op = Op(q, lambda e, o=out, i=in_: e.dma_start(out=o, in_=i))
        op.is_dma = True
        n = self.dma_n[q]
        self.dma_n[q] = n + 1
        slot = n % self.RING[q]
        op.sem = (q, slot)
        op.val = 16 * (n // self.RING[q] + 1)
        op.prev_slot = self.ring_last[q][slot]
        self.ring_last[q][slot] = op
        if op.prev_slot is not None:
            op.deps.add(op.prev_slot)
        if self.extra_dep is not None:
            op.deps.add(self.extra_dep)
        self._deps_and_record(op, in_, False)
        self._deps_and_record(op, out, True)
        self.ops[q].append(op)
        self.all_dmas.append(op)
        return op

    def mm(self, out, lhsT, rhs, start=True, stop=True):
        return self.add("pe", lambda e: e.matmul(out, lhsT, rhs, start=start, stop=stop),
                        reads=(lhsT, rhs), writes=(out,))

    def transpose(self, out, in_, ident):
        return self.add("pe", lambda e: e.transpose(out, in_, ident), reads=(in_, ident), writes=(out,))

    def act(self, out, in_, func, bias=None, scale=1.0, accum_out=None, eng="act"):
        def fn(e):
            kw = {}
            if bias is not None:
                kw["bias"] = bias
            if accum_out is not None:
                kw["accum_out"] = accum_out
            return e.activation(out=out, in_=in_, func=func, scale=scale, **kw)
        rd = [in_]
        if bias is not None and not isinstance(bias, (int, float)):
            rd.append(bias)
        if not isinstance(scale, (int, float)):
            rd.append(scale)
        wr = [out]
        if accum_out is not None:
            wr.append(accum_out)
        return self.add(eng, fn, reads=rd, writes=wr)

    def ts(self, eng, out, in0, s1, s2, op0, op1=None):
        def fn(e):
            if op1 is None:
                return e.tensor_scalar(out=out, in0=in0, scalar1=s1, scalar2=None, op0=op0)
            return e.tensor_scalar(out=out, in0=in0, scalar1=s1, scalar2=s2, op0=op0, op1=op1)
        rd = [in0] + [s for s in (s1, s2) if s is not None and not isinstance(s, (int, float))]
        return self.add(eng, fn, reads=rd, writes=(out,))

    def tt(self, eng, out, in0, in1, op):
        return self.add(eng, lambda e: e.tensor_tensor(out=out, in0=in0, in1=in1, op=op),
                        reads=(in0, in1), writes=(out,))

    def stt(self, out, in0, scalar, in1, op0, op1):
        rd = [in0, in1] + ([] if isinstance(scalar, (int, float)) else [scalar])
        return self.add("dve", lambda e: e.scalar_tensor_tensor(out=out, in0=in0, scalar=scalar, in1=in1,
                                                                op0=op0, op1=op1), reads=rd, writes=(out,))

    def copy(self, eng, out, in_):
        if eng == "act":
            return self.add("act", lambda e: e.copy(out=out, in_=in_), reads=(in_,), writes=(out,))
        return self.add(eng, lambda e: e.tensor_copy(out=out, in_=in_), reads=(in_,), writes=(out,))

    def memset(self, eng, out, val):
        return self.add(eng, lambda e: e.memset(out, val), reads=(), writes=(out,))

    def emit(self, semctx):
        nc = self.nc
        fin = Op("sp", None)
        for q in self.RING:
            for o in self.ring_last[q]:
                if o is not None:
                    fin.deps.add(o)
        self.ops["sp"].append(fin)
        for e in self.ENGS:
            for op in self.ops[e]:
                for d in op.deps:
                    d.sig = True
        esem = {}
        for e in ("pe", "act", "dve", "pool"):
            esem[e] = semctx(f"sem_{e}")
        rsem = {}
        for q, n in self.RING.items():
            for i in range(n):
                rsem[(q, i)] = semctx(f"ring_{q}_{i}")
        for e in ("pe", "act", "dve", "pool"):
            cnt = 0
            for op in self.ops[e]:
                if op.is_dma:
                    op.sem = rsem[op.sem]
                elif op.sig:
                    cnt += 1
                    op.sem = esem[e]
                    op.val = cnt
        for op in self.ops["sp"]:
            if op.is_dma:
                op.sem = rsem[op.sem]
        self.n_inst = {e: len(self.ops[e]) for e in self.ENGS}

        def run(eng_name, e):
            waited = {}
            for op in self.ops[eng_name]:
                needs = {}
                for d in op.deps:
                    k = id(d.sem)
                    if k not in needs or needs[k][1] < d.val:
                        needs[k] = (d.sem, d.val)
                for k, (sem, val) in needs.items():
                    if waited.get(k, 0) < val:
                        e.wait_ge(sem, val)
                        waited[k] = val
                if op.fn is None:
                    continue
                ins = op.fn(e)
                if op.is_dma:
                    ins.then_inc(op.sem, 16)
                elif op.sig:
                    ins.then_inc(op.sem, 1)

        with nc.Block() as block:
            @block.tensor
            def _(e):
                run("pe", e)

            @block.scalar
            def _(e):
                run("act", e)

            @block.vector
            def _(e):
                run("dve", e)

            @block.gpsimd
            def _(e):
                run("pool", e)

            @block.sync
            def _(e):
                run("sp", e)


def _fm_chunks():
    ch = []
    def seg(name, c0, n):
        for i in range(n // 128):
            ch.append((name, [(c0 + i * 128, 128)]))
    seg("sc_x", 0, 512); seg("sc_c", 512, 512); seg("sc_b", 1024, 512)
    seg("m_cq", 1536, 384); seg("m_ckv", 1920, 256)
    ch.append(("m_kr", [(2176, 64)]))
    ch.append(("m_krs", [(2208, 32), (2176, 32)]))
    seg("g_q", 2240, 512); seg("g_k", 2752, 512); seg("g_v", 3264, 512); seg("g_z", 3776, 512)
    seg("r_q", 4296, 512); seg("r_f", 4808, 512); seg("r_z", 5832, 512)
    return ch

FM = _fm_chunks()
NFM = len(FM)
FMI = {}
for _j, (_n, _s) in enumerate(FM):
    FMI.setdefault(_n, _j)
COL_RI = 5320
COL_GAB = 4288
NTM = 520

PV = {}
_o = 0
for _n, _w in (("g1", 16), ("g2", 16), ("scw", 12), ("scg", 4), ("mqg", 3), ("mkvg", 2), ("mog", 4),
               ("gcw", 48), ("gng", 1), ("hng", 1), ("alog", 4), ("dtb", 4)):
    PV[_n] = _o
    _o += _w
NV = _o

CO = {}
_o = 0
for _n, _w in (("ID", 128), ("TRI", 64), ("MU8", 512), ("NEGS8", 512), ("NEGM8", 512), ("I8", 512),
               ("CAUS", 128), ("ONES", 128), ("INVF", 1), ("SIGN", 1), ("SCANM", 2048)):
    CO[_n] = _o
    _o += _w
NCONST = _o


def make_consts():
    c = np.zeros((128, NCONST), np.float32)
    c[:, CO["ID"]:CO["ID"] + 128] = np.eye(128, dtype=np.float32)
    j = np.arange(64)[:, None]
    t = np.arange(64)[None, :]
    c[:64, CO["TRI"]:CO["TRI"] + 64] = (j <= t).astype(np.float32)
    mu = (j <= t).astype(np.float32)
    negs = -(j < t).astype(np.float32)
    negm = np.where(j <= t, 0.0, -30000.0).astype(np.float32)
    c[:64, CO["MU8"]:CO["MU8"] + 512] = np.tile(mu, (1, 8))
    c[:64, CO["NEGS8"]:CO["NEGS8"] + 512] = np.tile(negs, (1, 8))
    c[:64, CO["NEGM8"]:CO["NEGM8"] + 512] = np.tile(negm, (1, 8))
    c[:64, CO["I8"]:CO["I8"] + 512] = np.tile(np.eye(64, dtype=np.float32), (1, 8))
    k = np.arange(128)[:, None]
    q = np.arange(128)[None, :]
    c[:, CO["CAUS"]:CO["CAUS"] + 128] = (k <= q).astype(np.float32)
    c[:, CO["ONES"]:CO["ONES"] + 128] = 1.0
    half = 32
    invf = (10000.0 ** (-np.arange(half, dtype=np.float32) / half)).astype(np.float32)
    c[:64, CO["INVF"]] = np.concatenate([invf, invf])
    c[:64, CO["SIGN"]] = np.concatenate([-np.ones(32), np.ones(32)]).astype(np.float32)
    sm = np.ones(2048, np.float32)
    sm[::64] = 0.0
    c[:, CO["SCANM"]:CO["SCANM"] + 2048] = sm[None, :]
    return c


def colmaj(v, nchunk):
    v = np.asarray(v, np.float32)
    return np.ascontiguousarray(np.swapaxes(v.reshape(v.shape[:-1] + (nchunk, 128)), -1, -2))


def make_pvec(inp):
    pv = np.zeros((L, 128, NV), np.float32)
    for l in range(L):
        pv[l, :, PV["g1"]:PV["g1"] + 16] = colmaj(inp["norm1_g"][l], 16)
        pv[l, :, PV["g2"]:PV["g2"] + 16] = colmaj(inp["norm2_g"][l], 16)
        for j in range(3):
            pv[l, :, PV["scw"] + j * 4:PV["scw"] + j * 4 + 4] = colmaj(inp["sconv_w"][l, j], 4)
        pv[l, :, PV["scg"]:PV["scg"] + 4] = colmaj(inp["sconv_out_g"][l], 4)
        pv[l, :, PV["mqg"]:PV["mqg"] + 3] = colmaj(inp["mla_q_g"][l], 3)
        pv[l, :, PV["mkvg"]:PV["mkvg"] + 2] = colmaj(inp["mla_kv_g"][l], 2)
        pv[l, :, PV["mog"]:PV["mog"] + 4] = colmaj(inp["mla_out_g"][l], 4)
        for j in range(4):
            pv[l, :, PV["gcw"] + j * 12:PV["gcw"] + j * 12 + 12] = colmaj(inp["gdn_conv_w"][l, j], 12)
        pv[l, :, PV["gng"]] = np.asarray(inp["gdn_norm_g"][l], np.float32)
        pv[l, :, PV["hng"]] = np.asarray(inp["hgrn_norm_g"][l], np.float32)
        pv[l, :, PV["alog"]:PV["alog"] + 4] = np.asarray(inp["gdn_a_log"][l], np.float32)[None, :]
        pv[l, :, PV["dtb"]:PV["dtb"] + 4] = np.asarray(inp["gdn_dt_bias"][l], np.float32)[None, :]
    return pv


NAR = 48000


class Arena:
    def __init__(self, ar):
        self.ar = ar
        self.off = 0

    def reset(self, off=0):
        self.off = off

    def f(self, *shape):
        n = _prod(shape)
        v = self.ar[:, self.off:self.off + n]
        self.off += n
        assert self.off <= NAR, ("arena overflow", self.off)
        return self._shape(v, shape)

    def h(self, *shape):
        n = _prod(shape)
        nf = (n + 1) // 2
        v = self.ar[:, self.off:self.off + nf].bitcast(BF16)
        if 2 * nf != n:
            v = v[:, 0:n]
        self.off += nf
        assert self.off <= NAR, ("arena overflow", self.off)
        return self._shape(v, shape)

    @staticmethod
    def _shape(v, shape):
        if len(shape) == 1:
            return v
        if len(shape) == 2:
            return v.rearrange("p (a b) -> p a b", b=shape[1])
        if len(shape) == 3:
            return v.rearrange("p (a b c) -> p a b c", b=shape[1], c=shape[2])
        raise ValueError(shape)


class Builder:
    def __init__(self, nc, es, n_layers=L, n_seq=SEQ_PER_CORE, debug=None, phases=None):
        self.nc = nc
        self.es = es
        self.nl = n_layers
        self.ns = n_seq
        self.debug = debug or {}
        self.phases = phases
        self.P = Prog(nc)
        P = self.P
        ntok = n_seq * S
        self.ntok = ntok
        dt = nc.dram_tensor
        self.x = dt("x", [ntok, D], F32, kind="ExternalInput").ap()
        self.pos = dt("pos", [n_seq, S], I32, kind="ExternalInput").ap()
        self.w_in = dt("w_in", [L, D, DIN], F32, kind="ExternalInput").ap()
        self.w_o = dt("w_o", [L, D, D], F32, kind="ExternalInput").ap()
        self.w_ff1 = dt("w_ff1", [L, D, DFF], F32, kind="ExternalInput").ap()
        self.w_ff2 = dt("w_ff2", [L, DFF, D], F32, kind="ExternalInput").ap()
        self.w_uq = dt("w_uq", [L, 384, 768], F32, kind="ExternalInput").ap()
        self.w_ukv = dt("w_ukv", [L, 256, 1024], F32, kind="ExternalInput").ap()
        self.pvec = dt("pvec", [L, 128, NV], F32, kind="ExternalInput").ap()
        self.gbc = dt("gbc", [2 * L + 1, D], F32, kind="ExternalInput").ap()
        self.lbl = dt("lbl", [128, 16], F32, kind="ExternalInput").ap()
        self.consts = dt("consts", [128, NCONST], F32, kind="ExternalInput").ap()
        for n in ("x", "pos", "w_in", "w_o", "w_ff1", "w_ff2", "w_uq", "w_ukv", "pvec", "gbc", "lbl", "consts"):
            P.untracked.add(n)
        self.out = dt("out", [ntok, D], F32, kind="ExternalOutput").ap()
        def scr(name, shape, dtype):
            kind = "ExternalOutput" if name in self.debug else "Internal"
            return dt(name, shape, dtype, kind=kind).ap()
        self.H = scr("H", [ntok, D], F32)
        self.PT = scr("PT", [NFM * 128, S], F32)
        self.PNri = scr("PNri", [S, 512], F32)
        self.PNg = scr("PNg", [S, 8], F32)
        self.YT = scr("YT", [D, S], BF16)
        self.ROPE = scr("ROPE", [n_seq * 2 * 64, S], F32)
        self.WIN = [scr(f"WIN{l}", [NFM, 128, 2048], BF16) for l in range(n_layers)]
        self.WTM = [scr(f"WTM{l}", [128, 16 * NTM], BF16) for l in range(n_layers)]
        self.W1 = [scr(f"W1_{l}", [64, 128, 2048], BF16) for l in range(n_layers)]
        self.W2 = [scr(f"W2_{l}", [4 * 16, 128, 2048], BF16) for l in range(n_layers)]
        self.WO = [scr(f"WO_{l}", [4 * 4, 128, 2048], BF16) for l in range(n_layers)]
        sb = lambda name, shape, dtype: es.enter_context(nc.sbuf_tensor(name, shape, dtype))
        self.C32 = sb("C32", [128, NCONST - 2048], F32)
        self.C16 = sb("C16", [128, 384 + 2048], BF16)
        self.PVs = sb("PVs", [128, NV], F32)
        self.LB = sb("LB", [128, 49], F32)
        self.AR = sb("AR", [128, NAR], F32)
        self.ar = Arena(self.AR)
        self.ps = [es.enter_context(nc.psum_tensor(f"ps{i}", [128, 512], F32)) for i in range(8)]
        self._bank = 0
        self._eng_rr = 0

    def c32(self, name, n, rows=128):
        return self.C32[0:rows, CO[name]:CO[name] + n]

    def pv(self, name, i=0, n=1):
        return self.PVs[:, PV[name] + i:PV[name] + i + n]

    def nb(self):
        b = self.ps[self._bank]
        self._bank = (self._bank + 1) % 8
        return b

    def rr(self, engs=("act", "dve")):
        self._eng_rr += 1
        return engs[self._eng_rr % len(engs)]

    def evac(self, out, in_, eng=None):
        eng = eng or self.rr()
        return self.P.copy(eng, out, in_)

    def setup(self):
        P = self.P
        P.dma(self.C32[:, :], self.consts[:, 0:NCONST - 2048])
        ar = self.ar
        ar.reset()
        tmp = ar.f(2048)
        P.dma(tmp, self.consts[:, CO["SCANM"]:CO["SCANM"] + 2048])
        self.ID16 = self.C16[:, 0:128]
        self.ONES16 = self.C16[:, 128:256]
        self.CAUS16 = self.C16[:, 256:384]
        self.SCANM16 = self.C16[:, 384:384 + 2048]
        P.copy("dve", self.ID16, self.c32("ID", 128))
        P.copy("dve", self.ONES16, self.c32("ONES", 128))
        P.copy("dve", self.CAUS16, self.c32("CAUS", 128))
        P.copy("dve", self.SCANM16, tmp)
        self.ID32 = self.c32("ID", 128)
        lbt = ar.f(16)
        P.dma(lbt, self.lbl[:, :])
        e = ar.f(16)
        P.act(e, lbt, AF.Exp)
        s = ar.f(4)
        P.add("dve", lambda en: en.tensor_reduce(out=s, in_=e.rearrange("p (c l) -> p c l", l=4),
                                                 axis=AX.X, op=ALU.add), reads=(e,), writes=(s,))
        rs = ar.f(4)
        P.add("dve", lambda en: en.reciprocal(out=rs, in_=s), reads=(s,), writes=(rs,))
        pl = ar.f(16)
        for c in range(4):
            P.ts("dve", pl[:, c * 4:c * 4 + 4], e[:, c * 4:c * 4 + 4], rs[:, c:c + 1], None, ALU.mult)
        lb = self.LB[:, 0:16].rearrange("p (c l) -> p c l", l=4)
        pl3 = pl.rearrange("p (c l) -> p c l", l=4)
        P.memset("dve", lb[:, :, 0:1], 0.0)
        for l in range(1, 4):
            P.tt("dve", lb[:, :, l:l + 1], lb[:, :, l - 1:l], pl3[:, :, l:l + 1], ALU.add)
        P.ts("dve", self.LB[:, 16:32], self.LB[:, 0:16], -1.0, 1.0, ALU.mult, ALU.add)
        P.ts("dve", self.LB[:, 32:48], self.LB[:, 0:16], 1.0, -1.0, ALU.mult, ALU.add)
        self.EPSC = self.LB[:, 48:49]
        P.memset("dve", self.EPSC, EPS)

    def load_layer_params(self, l):
        self.P.dma(self.PVs[:, :], self.pvec[l])

    def convert_items(self, l):
        items = []
        P = self.P
        w_in = self.w_in[l].rearrange("(kc p) c -> p kc c", p=128)
        for j, (_, segs) in enumerate(FM):
            def it(j=j, segs=segs):
                dst = self.WIN[l][j].rearrange("p (kc c) -> p kc c", c=128)
                o = 0
                for (c0, n) in segs:
                    P.dma(dst[:, :, o:o + n], w_in[:, :, c0:c0 + n], q="pool")
                    o += n
            items.append(it)
        wtm = self.WTM[l].rearrange("p (kc c) -> p kc c", c=NTM)
        for c_lo in range(0, 512, 128):
            items.append(lambda c_lo=c_lo: P.dma(wtm[:, :, c_lo:c_lo + 128], w_in[:, :, COL_RI + c_lo:COL_RI + c_lo + 128], q="pool"))
        items.append(lambda: P.dma(wtm[:, :, 512:520], w_in[:, :, COL_GAB:COL_GAB + 8], q="pool"))
        w1 = self.w_ff1[l].rearrange("(kc p) c -> p kc c", p=128)
        for f in range(64):
            items.append(lambda f=f: P.dma(self.W1[l][f].rearrange("p (kc c) -> p kc c", c=128),
                                           w1[:, :, f * 128:(f + 1) * 128], q="pool"))
        for (src, dst, nchunk) in ((self.w_ff2[l], self.W2[l], 64), (self.w_o[l], self.WO[l], 16)):
            nfg = nchunk // 4
            dstv = dst.rearrange("(q fg) p (fi c) -> p q fg fi c", fg=nfg, c=512)
            for f in range(nchunk):
                items.append(lambda f=f, src=src, dstv=dstv: P.dma(
                    dstv[:, :, f // 4, f % 4, :], src[f * 128:(f + 1) * 128, :].rearrange("p (q c) -> p q c", c=512), q="pool"))
        return items

    def convert_now(self, items):
        for it in items:
            it()

    def pump_alloc(self):
        pass

    def pump(self, n=1, after=None):
        for _ in range(n):
            if not self.convq:
                return
            self.P.extra_dep = after
            self.convq.pop(0)()
            self.P.extra_dep = None

    def norm_to_fm(self, h32, gbc, hn2, junk, ss, outT):
        P = self.P
        for b in range(4):
            P.act(junk, h32[:, b, :], AF.Square, accum_out=ss[:, b:b + 1])
        P.ts("dve", ss[:, 4:8], ss[:, 0:4], 1.0 / D, EPS, ALU.mult, ALU.add)
        P.act(ss[:, 4:8], ss[:, 4:8], AF.Sqrt)
        P.add("dve", lambda e: e.reciprocal(out=ss[:, 8:12], in_=ss[:, 4:8]), reads=(ss[:, 4:8],), writes=(ss[:, 8:12],))
        for b in range(4):
            hn = hn2[b % 2]
            P.stt(hn, h32[:, b, :], ss[:, 8 + b:9 + b], gbc, ALU.mult, ALU.mult)
            for cg in range(4):
                bank = self.nb()
                for ci in range(4):
                    c = cg * 4 + ci
                    P.transpose(bank[:, ci * 128:(ci + 1) * 128], hn[:, c * 128:(c + 1) * 128], self.ID32)
                self.evac(outT[:, cg * 4:cg * 4 + 4, b * 128:(b + 1) * 128],
                          bank[:, :].rearrange("p (a b) -> p a b", b=128))

    def phase1(self, l, s):
        P = self.P
        ar = self.ar
        ar.reset()
        gbc = ar.f(2048)
        P.dma(gbc, self.gbc[l:l + 1, :].partition_broadcast(128))
        h32 = ar.f(4, 2048)
        hn2 = [ar.f(2048), ar.f(2048)]
        junk = ar.h(2048)
        sss = [ar.f(12), ar.f(12)]
        uTs = [ar.h(16, 512) for _ in range(4)]
        wtm = ar.h(16, NTM)
        slabs = [ar.h(16, 128) for _ in range(3)]
        stg = [ar.f(512) for _ in range(4)]
        stg8 = [ar.f(8) for _ in range(2)]
        P.dma(wtm, self.WTM[l].rearrange("p (kc c) -> p kc c", c=NTM))
        srcH = self.x if l == 0 else self.H
        nst = 0

        def prep(tt):
            tok0 = s * S + tt * 512
            P.dma(h32, srcH[tok0:tok0 + 512, :].rearrange("(b p) d -> p b d", p=128))
            self.norm_to_fm(h32, gbc, hn2, junk, sss[tt % 2], uTs[tt % 4])
        prep(0)
        prep(1)
        for pair in range(2):
            tiles = (2 * pair, 2 * pair + 1)
            PRE = 2
            for j in range(min(PRE, NFM)):
                P.dma(slabs[j % 3], self.WIN[l][j].rearrange("p (kc c) -> p kc c", c=128))
            for j in range(NFM):
                if j + PRE < NFM:
                    P.dma(slabs[(j + PRE) % 3], self.WIN[l][j + PRE].rearrange("p (kc c) -> p kc c", c=128))
                width = sum(n for _, n in FM[j][1])
                slab = slabs[j % 3]
                for tt in tiles:
                    uT = uTs[tt % 4]
                    bank = self.nb()
                    for kc in range(16):
                        mmop = P.mm(bank[0:width, :], slab[:, kc, 0:width], uT[:, kc, :], start=(kc == 0), stop=(kc == 15))
                    st = stg[nst % 4]
                    nst += 1
                    self.evac(st[0:width, :], bank[0:width, :])
                    P.dma(self.PT[j * 128:j * 128 + width, tt * 512:(tt + 1) * 512], st[0:width, :])
                    if l == 0 and s == 0:
                        self.pump(1, after=mmop)
                if j == 8 and pair == 0:
                    prep(2)
                    prep(3)
            for tt in tiles:
                uT = uTs[tt % 4]
                for b in range(4):
                    bank = self.nb()
                    for kc in range(16):
                        P.mm(bank[:, :], uT[:, kc, b * 128:(b + 1) * 128], wtm[:, kc, 0:512], start=(kc == 0), stop=(kc == 15))
                    st = stg[nst % 4]
                    nst += 1
                    self.evac(st, bank[:, :])
                    r0 = tt * 512 + b * 128
                    P.dma(self.PNri[r0:r0 + 128, :], st)
                    bank = self.nb()
                    for kc in range(16):
                        P.mm(bank[:, 0:8], uT[:, kc, b * 128:(b + 1) * 128], wtm[:, kc, 512:520], start=(kc == 0), stop=(kc == 15))
                    s8 = stg8[b % 2]
                    self.evac(s8, bank[:, 0:8])
                    P.dma(self.PNg[r0:r0 + 128, :], s8)

    def gemm_tm_add(self, h32, aT, Wc, nchunk, pieces):
        P = self.P
        nfg = nchunk // 4
        for q in range(4):
            banks = self.ps[0:4] if q % 2 == 0 else self.ps[4:8]
            for fg in range(nfg):
                pc = pieces[(q * nfg + fg) % len(pieces)]
                P.dma(pc, Wc[q * nfg + fg].rearrange("p (fi c) -> p fi c", c=512))
                for fi in range(4):
                    f = fg * 4 + fi
                    for b in range(4):
                        P.mm(banks[b][:, :], aT[:, f, b * 128:(b + 1) * 128], pc[:, fi, :],
                             start=(f == 0), stop=(f == nchunk - 1))
            for b in range(4):
                dst = h32[:, b, q * 512:(q + 1) * 512]
                P.tt("dve", dst, banks[b][:, :], dst, ALU.add)

    def phase3(self, l, s):
        P = self.P
        ar = self.ar
        ar.reset()
        gbc = ar.f(2048)
        P.dma(gbc, self.gbc[L + l:L + l + 1, :].partition_broadcast(128))
        last = (l == self.nl - 1)
        if last:
            gfin = ar.f(2048)
            P.dma(gfin, self.gbc[2 * L:2 * L + 1, :].partition_broadcast(128))
        h32 = ar.f(4, 2048)
        hn2 = [ar.f(2048), ar.f(2048)]
        junk = ar.h(2048)
        ss = ar.f(12)
        r32 = [ar.f(512) for _ in range(2)]
        vT = ar.h(16, 512)
        aT = ar.h(64, 512)
        slabs = [ar.h(16, 128) for _ in range(3)]
        pieces = [ar.h(4, 512) for _ in range(3)]
        if not last:
            self.pump_alloc()
        srcH = self.x if l == 0 else self.H
        for tt in range(4):
            tok0 = s * S + tt * 512
            P.dma(h32, srcH[tok0:tok0 + 512, :].rearrange("(b p) d -> p b d", p=128))
            P.dma(vT, self.YT[:, tt * 512:(tt + 1) * 512].rearrange("(c p) t -> p c t", p=128))
            self.gemm_tm_add(h32, vT, self.WO[l], 16, pieces)
            self.norm_to_fm(h32, gbc, hn2, junk, ss, vT)
            PRE = 2
            for f in range(PRE):
                P.dma(slabs[f % 3], self.W1[l][f].rearrange("p (kc c) -> p kc c", c=128))
            for f in range(64):
                if f + PRE < 64:
                    P.dma(slabs[(f + PRE) % 3], self.W1[l][f + PRE].rearrange("p (kc c) -> p kc c", c=128))
                slab = slabs[f % 3]
                bank = self.nb()
                for kc in range(16):
                    mmop = P.mm(bank[:, :], slab[:, kc, :], vT[:, kc, :], start=(kc == 0), stop=(kc == 15))
                if not last and f % 4 == 0:
                    self.pump(1, after=mmop)
                r = r32[f % 2]
                P.act(r, bank[:, :], AF.Relu)
                P.tt("dve", aT[:, f, :], r, r, ALU.mult)
            self.gemm_tm_add(h32, aT, self.W2[l], 64, pieces)
            if last:
                for b in range(4):
                    P.act(junk, h32[:, b, :], AF.Square, accum_out=ss[:, b:b + 1])
                P.ts("dve", ss[:, 4:8], ss[:, 0:4], 1.0 / D, EPS, ALU.mult, ALU.add)
                P.act(ss[:, 4:8], ss[:, 4:8], AF.Sqrt)
                P.add("dve", lambda e: e.reciprocal(out=ss[:, 8:12], in_=ss[:, 4:8]), reads=(ss[:, 4:8],), writes=(ss[:, 8:12],))
                for b in range(4):
                    P.stt(h32[:, b, :], h32[:, b, :], ss[:, 8 + b:9 + b], gfin, ALU.mult, ALU.mult)
                P.dma(self.out[tok0:tok0 + 512, :].rearrange("(b p) d -> p b d", p=128), h32)
            else:
                P.dma(self.H[tok0:tok0 + 512, :].rearrange("(b p) d -> p b d", p=128), h32)

    def yt_from_pt(self):
        P = self.P
        ar = self.ar
        ar.reset()
        t32 = [ar.f(2048) for _ in range(2)]
        t16 = [ar.h(2048) for _ in range(2)]
        for c in range(16):
            P.dma(t32[c % 2], self.PT[c * 128:(c + 1) * 128, :])
            P.copy("dve", t16[c % 2], t32[c % 2])
            P.dma(self.YT[c * 128:(c + 1) * 128, :], t16[c % 2])

    def build(self, mixers=True):
        self.setup()
        for l in range(self.nl):
            self.load_layer_params(l)
            self.convert_layer(l)
            for s in range(self.ns):
                self.phase1(l, s)
                if mixers:
                    self.mix_sconv(l, s)
                    self.mix_mla(l, s)
                    self.mix_gdn(l, s)
                    self.mix_hgrn(l, s)
                else:
                    self.yt_from_pt()
                self.phase3(l, s)
        self.P.emit(lambda name: self.es.enter_context(self.nc.semaphore(name)))


def host_inputs(inp, core, n_seq=SEQ_PER_CORE):
    f = lambda a: np.ascontiguousarray(np.asarray(a, np.float32))
    b0 = core * n_seq
    x = f(inp["x"][b0:b0 + n_seq]).reshape(n_seq * S, D)
    pos = np.ascontiguousarray(np.asarray(inp["positions"][b0:b0 + n_seq], np.int32))
    return {"x": x, "pos": pos}


def shared_inputs(inp):
    f = lambda a: np.ascontiguousarray(np.asarray(a, np.float32))
    gbc = np.concatenate([f(inp["norm1_g"]), f(inp["norm2_g"]), f(inp["final_g"])[None, :]], axis=0)
    lbl = np.ascontiguousarray(np.transpose(f(inp["hgrn_lb_logits"]).reshape(L, 4, 128), (2, 1, 0)).reshape(128, 16))
    return {
        "w_in": f(inp["w_in"]), "w_o": f(inp["w_o"]), "w_ff1": f(inp["w_ff1"]), "w_ff2": f(inp["w_ff2"]),
        "w_uq": f(inp["mla_w_uq"]), "w_ukv": f(inp["mla_w_ukv"]),
        "pvec": make_pvec(inp), "gbc": gbc, "lbl": lbl, "consts": make_consts(),
    }


def kernel(**inp):
    from contextlib import ExitStack
    nc = bass.Bass("TRN2", target_bir_lowering=False)
    with ExitStack() as es:
        b = Builder(nc, es)
        b.build()
    sh = shared_inputs(inp)
    in_maps = []
    for c in range(NCORES):
        m = dict(sh)
        m.update(host_inputs(inp, c))
        in_maps.append(m)
    res = run_bass_kernel_spmd(nc, in_maps, core_ids=list(range(NCORES)))
    outs = [res.results[c]["out"].reshape(SEQ_PER_CORE, S, D) for c in range(NCORES)]
    return np.concatenate(outs, axis=0).astype(np.float32)


SCALE_ATT = (128 + 64) ** -0.5
TSOLVE_BF16 = False
TWO_PI = 2.0 * math.pi
C1 = 6.28125
C2 = TWO_PI - C1


def _rstd_from(self, out, ss_ap, n, tmp):
    P = self.P
    P.act(tmp, ss_ap, AF.Sqrt, bias=self.EPSC, scale=1.0 / n)
    P.add("dve", lambda e: e.reciprocal(out=out, in_=tmp), reads=(tmp,), writes=(out,))


def _pt_rows(self, name, c=0, rows=128):
    j = FMI[name] + c
    return self.PT[j * 128:j * 128 + rows, :]


def setup_rope(self):
    P = self.P
    ar = self.ar
    PI = math.pi
    for s in range(self.ns):
        ar.reset()
        posi = ar.f(2048)[0:64, :].bitcast(I32)
        P.dma(posi, self.pos[s:s + 1, :].partition_broadcast(64))
        ang = ar.f(2048)[0:64, :]
        P.copy("dve", ang, posi)
        P.ts("dve", ang, ang, self.c32("INVF", 1, 64), None, ALU.mult)
        kf = ar.f(2048)[0:64, :]
        ki = ar.f(2048)[0:64, :].bitcast(I32)
        P.ts("dve", kf, ang, 1.0 / TWO_PI, None, ALU.mult)
        P.copy("dve", ki, kf)
        P.copy("dve", kf, ki)
        r = ar.f(2048)[0:64, :]
        P.stt(r, kf, -C1, ang, ALU.mult, ALU.add)
        P.stt(r, kf, -C2, r, ALU.mult, ALU.add)
        m = ar.f(2048)[0:64, :]

        def fold(v):
            P.ts("dve", m, v, PI, -TWO_PI, ALU.is_gt, ALU.mult)
            P.tt("dve", v, v, m, ALU.add)
            P.ts("dve", m, v, -PI, TWO_PI, ALU.is_lt, ALU.mult)
            P.tt("dve", v, v, m, ALU.add)
            P.ts("dve", v, v, -3.14159, 3.14159, ALU.max, ALU.min)
        fold(r)
        fold(r)
        sn = ar.f(2048)[0:64, :]
        P.act(sn, r, AF.Sin, scale=self.c32("SIGN", 1, 64))
        P.dma(self.ROPE[(s * 2 + 1) * 64:(s * 2 + 2) * 64, :], sn)
        rc = ar.f(2048)[0:64, :]
        P.ts("dve", rc, r, PI / 2, None, ALU.add)
        fold(rc)
        cs = ar.f(2048)[0:64, :]
        P.act(cs, rc, AF.Sin)
        P.dma(self.ROPE[(s * 2) * 64:(s * 2 + 1) * 64, :], cs)


def mix_sconv(self, l, s):
    P = self.P
    ar = self.ar
    ar.reset()
    y = ar.f(4, 2048)
    sq = ar.h(4, 2048)
    zp = [ar.f(2050) for _ in range(2)]
    xin = [ar.f(2048) for _ in range(2)]
    cin = [ar.f(2048) for _ in range(2)]
    bin_ = [ar.f(2048) for _ in range(2)]
    rstd = ar.f(2048)
    tmp = ar.f(512)
    yo = [ar.h(2048) for _ in range(2)]
    for c in range(4):
        x_, c_, b_, z_ = xin[c % 2], cin[c % 2], bin_[c % 2], zp[c % 2]
        P.dma(x_, _pt_rows(self, "sc_x", c))
        P.dma(c_, _pt_rows(self, "sc_c", c))
        P.dma(b_, _pt_rows(self, "sc_b", c))
        P.memset("pool", z_[:, 0:2], 0.0)
        P.tt("pool", z_[:, 2:2050], x_, c_, ALU.mult)
        acc = y[:, c, :]
        P.act(acc, z_[:, 2:2050], AF.Copy, scale=self.pv("scw", 2 * 4 + c))
        P.stt(acc, z_[:, 1:2049], self.pv("scw", 1 * 4 + c), acc, ALU.mult, ALU.add)
        P.stt(acc, z_[:, 0:2048], self.pv("scw", 0 * 4 + c), acc, ALU.mult, ALU.add)
        P.tt("pool", acc, acc, b_, ALU.mult)
        P.act(sq[:, c, :], acc, AF.Square)
    for tt in range(4):
        bank = self.nb()
        sl = slice(tt * 512, (tt + 1) * 512)
        for c in range(4):
            P.mm(bank[:, :], self.ONES16, sq[:, c, sl], start=(c == 0), stop=(c == 3))
        _rstd_from(self, rstd[:, sl], bank[:, :], 512, tmp)
    for c in range(4):
        o = yo[c % 2]
        P.stt(o, y[:, c, :], self.pv("scg", c), rstd, ALU.mult, ALU.mult)
        P.dma(self.YT[c * 128:(c + 1) * 128, :], o)


def mix_mla(self, l, s):
    P = self.P
    ar = self.ar
    ar.reset()
    wq = ar.h(3, 768)
    wqs = ar.h(3, 256)
    wkv = ar.h(2, 1024)
    wv = ar.h(2, 512)
    mark = ar.off
    w32q = ar.f(3, 768)
    w32kv = ar.f(2, 1024)
    P.dma(w32q, self.w_uq[l].rearrange("(kc p) c -> p kc c", p=128))
    P.dma(w32kv, self.w_ukv[l].rearrange("(kc p) c -> p kc c", p=128))
    P.copy("dve", wq, w32q)
    P.copy("pool", wkv, w32kv)
    for kc in range(3):
        src = w32q[:, kc, :].rearrange("p (h d) -> p h d", d=192)
        dst = wqs[:, kc, :].rearrange("p (h d) -> p h d", d=64)
        P.copy("dve", dst[:, :, 0:32], src[:, :, 160:192])
        P.copy("dve", dst[:, :, 32:64], src[:, :, 128:160])
    for kc in range(2):
        src = w32kv[:, kc, :].rearrange("p (h d) -> p h d", d=256)
        P.copy("pool", wv[:, kc, :].rearrange("p (h d) -> p h d", d=128), src[:, :, 128:256])
    ar.reset(mark)
    qn = ar.h(4, 2048)
    qr = ar.h(4, 2048)
    kn = ar.h(4, 2048)
    kro = ar.h(2048)
    vtm = ar.h(16, 512)
    cos2 = ar.f(2048)
    sin2 = ar.f(2048)
    P.dma(cos2[0:64, :], self.ROPE[(s * 2) * 64:(s * 2 + 1) * 64, :])
    P.dma(sin2[0:64, :], self.ROPE[(s * 2 + 1) * 64:(s * 2 + 2) * 64, :])
    mark = ar.off
    cqn = ar.h(3, 2048)
    ckvn = ar.h(2, 2048)
    mark2 = ar.off
    cq = ar.f(3, 2048)
    sq = ar.h(3, 2048)
    rstd = ar.f(512)
    tmp = ar.f(512)
    for (name, nch, gname, dst) in (("m_cq", 3, "mqg", cqn), ("m_ckv", 2, "mkvg", ckvn)):
        for c in range(nch):
            P.dma(cq[:, c, :], _pt_rows(self, name, c))
            P.act(sq[:, c, :], cq[:, c, :], AF.Square)
        for tt in range(4):
            sl = slice(tt * 512, (tt + 1) * 512)
            bank = self.nb()
            for c in range(nch):
                P.mm(bank[:, :], self.ONES16, sq[:, c, sl], start=(c == 0), stop=(c == nch - 1))
            _rstd_from(self, rstd, bank[:, :], nch * 128, tmp)
            for c in range(nch):
                P.stt(dst[:, c, sl], cq[:, c, sl], self.pv(gname, c), rstd, ALU.mult, ALU.mult)
    ar.reset(mark2)
    kr = ar.f(2048)
    krs = ar.f(2048)
    P.dma(kr[0:64, :], _pt_rows(self, "m_kr", 0, 64))
    P.dma(krs[0:64, :], _pt_rows(self, "m_krs", 0, 64))
    P.tt("dve", kr[0:64, :], kr[0:64, :], cos2[0:64, :], ALU.mult)
    P.tt("pool", krs[0:64, :], krs[0:64, :], sin2[0:64, :], ALU.mult)
    P.tt("dve", kro[0:64, :], kr[0:64, :], krs[0:64, :], ALU.add)
    t1 = ar.f(512)
    t2 = ar.f(512)
    for tt in range(4):
        sl = slice(tt * 512, (tt + 1) * 512)
        for h in range(4):
            bank = self.nb()
            for kc in range(3):
                P.mm(bank[:, :], wq[:, kc, h * 192:h * 192 + 128], cqn[:, kc, sl], start=(kc == 0), stop=(kc == 2))
            self.evac(qn[:, h, sl], bank[:, :])
            bx = self.nb()
            for kc in range(3):
                P.mm(bx[0:64, :], wq[:, kc, h * 192 + 128:h * 192 + 192], cqn[:, kc, sl], start=(kc == 0), stop=(kc == 2))
            bxs = self.nb()
            for kc in range(3):
                P.mm(bxs[0:64, :], wqs[:, kc, h * 64:(h + 1) * 64], cqn[:, kc, sl], start=(kc == 0), stop=(kc == 2))
            P.tt("dve", t1[0:64, :], bx[0:64, :], cos2[0:64, sl], ALU.mult)
            P.tt("dve", t2[0:64, :], bxs[0:64, :], sin2[0:64, sl], ALU.mult)
            P.tt("pool", qr[0:64, h, sl], t1[0:64, :], t2[0:64, :], ALU.add)
            bank = self.nb()
            for kc in range(2):
                P.mm(bank[:, :], wkv[:, kc, h * 256:h * 256 + 128], ckvn[:, kc, sl], start=(kc == 0), stop=(kc == 1))
            self.evac(kn[:, h, sl], bank[:, :])
    for blk in range(16):
        bank = self.nb()
        for kc in range(2):
            P.mm(bank[:, :], ckvn[:, kc, blk * 128:(blk + 1) * 128], wv[:, kc, :], start=(kc == 0), stop=(kc == 1))
        self.evac(vtm[:, blk, :], bank[:, :])
    ar.reset(mark)
    pbuf = [ar.h(512) for _ in range(3)]
    o32 = ar.f(4, 512)
    rl = ar.f(512)
    sqo = ar.h(4, 512)
    rstd = ar.f(512)
    tmp = ar.f(512)
    ybuf = [ar.h(4, 512) for _ in range(2)]
    sb_i = 0
    pb_i = 0
    acc_i = 0
    for tt in range(4):
        q0 = tt * 512
        for h in range(4):
            Ob = self.ps[4 + acc_i % 2]
            Lb = self.ps[6 + acc_i % 2]
            acc_i += 1
            nkb = 4 * tt + 4
            Sbs = {}

            def issue_S(kb):
                nonlocal sb_i
                j = kb - 4 * tt
                c0 = 128 * j if j > 0 else 0
                Sb = self.ps[sb_i % 4]
                sb_i += 1
                ks = slice(kb * 128, (kb + 1) * 128)
                P.mm(Sb[:, c0:512], kn[:, h, ks], qn[:, h, q0 + c0:q0 + 512], start=True, stop=False)
                P.mm(Sb[:, c0:512], kro[0:64, ks], qr[0:64, h, q0 + c0:q0 + 512], start=False, stop=True)
                Sbs[kb] = (Sb, c0, j)
            issue_S(0)
            for kb in range(nkb):
                if kb + 1 < nkb:
                    issue_S(kb + 1)
                Sb, c0, j = Sbs.pop(kb)
                pT = pbuf[pb_i % 3]
                pb_i += 1
                P.act(pT[:, c0:512], Sb[:, c0:512], AF.Exp, scale=SCALE_ATT)
                if j >= 0:
                    P.tt("pool", pT[:, c0:c0 + 128], pT[:, c0:c0 + 128], self.CAUS16, ALU.mult)
                P.mm(Ob[:, c0:512], vtm[:, kb, h * 128:(h + 1) * 128], pT[:, c0:512], start=(kb == 0), stop=(kb == nkb - 1))
                P.mm(Lb[:, c0:512], self.ONES16, pT[:, c0:512], start=(kb == 0), stop=(kb == nkb - 1))
            self.pump(1, after=P.ops["pe"][-1])
            P.add("dve", lambda e, Lb=Lb: e.reciprocal(out=rl, in_=Lb[:, :]), reads=(Lb[:, :],), writes=(rl,))
            P.tt("dve", o32[:, h, :], Ob[:, :], rl, ALU.mult)
            P.act(sqo[:, h, :], o32[:, h, :], AF.Square)
        bank = self.ps[sb_i % 4]
        sb_i += 1
        for h in range(4):
            P.mm(bank[:, :], self.ONES16, sqo[:, h, :], start=(h == 0), stop=(h == 3))
        _rstd_from(self, rstd, bank[:, :], 512, tmp)
        yb = ybuf[tt % 2]
        for h in range(4):
            P.stt(yb[:, h, :], o32[:, h, :], self.pv("mog", h), rstd, ALU.mult, ALU.mult)
        P.dma(self.YT[512:1024, q0:q0 + 512].rearrange("(h p) t -> p h t", p=128), yb)


Builder.setup_rope = setup_rope
Builder.mix_sconv = mix_sconv
Builder.mix_mla = mix_mla


def build2(self, mixers=("sconv", "mla", "gdn", "hgrn")):
    self.setup()
    self.setup_rope()
    items0 = self.convert_items(0)
    self.convert_now(items0[:NFM + 5])
    self.convq = items0[NFM + 5:]
    for l in range(self.nl):
        self.load_layer_params(l)
        if self.convq and l > 0:
            self.convert_now(self.convq)
            self.convq = []
        for s in range(self.ns):
            self.phase1(l, s)
            if s == 0 and l + 1 < self.nl:
                if self.convq:
                    self.convert_now(self.convq)
                self.convq = self.convert_items(l + 1)
            if mixers is None:
                self.yt_from_pt()
            else:
                for m in mixers:
                    getattr(self, "mix_" + m)(l, s)
            self.phase3(l, s)
    self.P.emit(lambda name: self.es.enter_context(self.nc.semaphore(name)))


Builder.build = build2


def mix_hgrn(self, l, s):
    P = self.P
    ar = self.ar
    ar.reset()
    o32 = ar.h(4, 2048)
    mark0 = ar.off
    qeT = ar.h(4, 2048)
    keT = ar.h(4, 2048)
    qsT = ar.h(4, 2048)
    kdtm = ar.h(4, 32 * 128)
    itm = ar.h(32, 512)
    ebl = ar.f(4, 32)
    S32 = ar.f(4, 128)
    S16 = ar.h(4, 128)
    mark = ar.off
    stg = [ar.f(4, 512) for _ in range(2)]
    src = self.PNri.rearrange("(n p) f -> p n f", p=64)
    for g in range(8):
        st = stg[g % 2]
        P.dma(st[0:64], src[:, g * 4:(g + 1) * 4, :])
        P.copy(self.rr(("act", "pool")), itm[0:64, g * 4:(g + 1) * 4, :], st[0:64])
    ar.reset(mark)
    t_qs = [ar.f(2048), ar.f(2048)]
    t_ks = [ar.f(2048), ar.f(2048)]
    t_d = ar.f(2048)
    t_bc = ar.f(2048)
    t_e = ar.f(2048)
    def v3(t):
        return t.rearrange("p (n c) -> p n c", c=64)
    for h in range(4):
        t_q, t_k = t_qs[h % 2], t_ks[h % 2]
        lbc = self.LB[:, h * 4 + l:h * 4 + l + 1]
        oml = self.LB[:, 16 + h * 4 + l:16 + h * 4 + l + 1]
        noml = self.LB[:, 32 + h * 4 + l:32 + h * 4 + l + 1]
        P.dma(t_q, _pt_rows(self, "r_q", h))
        P.dma(t_k, _pt_rows(self, "r_f", h))
        P.act(t_q, t_q, AF.Silu)
        P.act(t_k, t_k, AF.Sigmoid)
        P.act(t_d, t_k, AF.Ln, bias=lbc, scale=oml)
        P.ts("pool", t_k, t_k, noml, oml, ALU.mult, ALU.add)
        P.add("dve", lambda e: e.tensor_tensor_scan(out=t_bc, data0=self.SCANM16, data1=t_d, initial=0.0,
                                                    op0=ALU.mult, op1=ALU.add),
              reads=(self.SCANM16, t_d), writes=(t_bc,))
        bc3 = v3(t_bc)
        P.act(ebl[:, h, :], bc3[:, :, 63], AF.Exp)
        P.tt("dve", v3(t_d), bc3, bc3[:, :, 31:32].broadcast_to([128, 32, 64]), ALU.subtract)
        P.act(t_e, t_d, AF.Exp)
        P.tt("dve", qeT[:, h, :], t_q, t_e, ALU.mult)
        P.act(t_e, t_d, AF.Exp, scale=-1.0)
        P.tt("pool", keT[:, h, :], t_k, t_e, ALU.mult)
        P.act(t_e, t_bc, AF.Exp)
        P.tt("dve", qsT[:, h, :], t_q, t_e, ALU.mult)
        P.tt("dve", v3(t_d), bc3[:, :, 63:64].broadcast_to([128, 32, 64]), bc3, ALU.subtract)
        P.act(t_e, t_d, AF.Exp)
        P.tt("pool", t_d, t_k, t_e, ALU.mult)
        for g in range(8):
            bank = self.ps[g % 4]
            for i in range(4):
                n = g * 4 + i
                P.transpose(bank[0:64, i * 128:(i + 1) * 128], t_d[:, n * 64:(n + 1) * 64], self.ID32)
            self.evac(kdtm[0:64, h, g * 512:(g + 1) * 512], bank[0:64, :])
        self.pump(2, after=P.ops["pe"][-1])
    ar.reset(mark)
    attnT = ar.h(16, 512)
    MU8 = self.c32("MU8", 512, 64)
    for blk in range(16):
        bank = self.ps[blk % 4]
        for ci in range(2):
            n = blk * 2 + ci
            cs = slice(n * 64, (n + 1) * 64)
            for h in range(4):
                col = (ci * 4 + h) * 64
                P.mm(bank[0:64, col:col + 64], keT[:, h, cs], qeT[:, h, cs], start=True, stop=True)
        P.tt("dve", attnT[0:64, blk, :], bank[0:64, :], MU8, ALU.mult)
    P.memset("dve", S32, 0.0)
    P.memset("pool", S16, 0.0)
    for blk in range(16):
        Ob = self.ps[4 + blk % 2]
        for ci in range(2):
            n = blk * 2 + ci
            cs = slice(n * 64, (n + 1) * 64)
            Sb = self.ps[6 + n % 2]
            for h in range(4):
                ocol = (h * 2 + ci) * 64
                acol = (ci * 4 + h) * 64
                iv = itm[0:64, n, h * 128:(h + 1) * 128]
                P.mm(Ob[:, ocol:ocol + 64], S16[:, h, :], qsT[:, h, cs], start=True, stop=False)
                P.mm(Ob[:, ocol:ocol + 64], iv, attnT[0:64, blk, acol:acol + 64], start=False, stop=True)
                P.mm(Sb[:, h * 128:(h + 1) * 128], kdtm[0:64, h, n * 128:(n + 1) * 128], iv, start=True, stop=True)
            for h in range(4):
                P.stt(S32[:, h, :], S32[:, h, :], ebl[:, h, n:n + 1], Sb[:, h * 128:(h + 1) * 128], ALU.mult, ALU.add)
            P.copy("act", S16, S32)
        P.copy("pool" if False else "act", o32[:, :, blk * 128:(blk + 1) * 128],
               Ob[:, :].rearrange("p (h t) -> p h t", t=128))
    ar.reset(mark0)
    _out_norm_gate(self, o32, "hng", "r_z", AF.Sigmoid, 1536)


Builder.mix_hgrn = mix_hgrn


def _out_norm_gate(self, o, gname, zname, gate_func, row0):
    P = self.P
    ar = self.ar
    sqs = [ar.h(2048) for _ in range(2)]
    rstds = [ar.f(2048) for _ in range(2)]
    tmps = [ar.f(512) for _ in range(2)]
    rzs = [ar.f(2048) for _ in range(2)]
    t_ys = [ar.f(2048) for _ in range(2)]
    yo = [ar.h(2048) for _ in range(2)]
    for h in range(4):
        rz, sq, rstd, t_y, tmp = rzs[h % 2], sqs[h % 2], rstds[h % 2], t_ys[h % 2], tmps[h % 2]
        P.dma(rz, _pt_rows(self, zname, h))
        P.act(rz, rz, gate_func)
        P.act(sq, o[:, h, :], AF.Square)
        for tt in range(4):
            sl = slice(tt * 512, (tt + 1) * 512)
            bank = self.ps[(h * 4 + tt) % 8]
            P.mm(bank[:, :], self.ONES16, sq[:, sl], start=True, stop=True)
            _rstd_from(self, rstd[:, sl], bank[:, :], 128, tmp)
        P.stt(t_y, o[:, h, :], self.pv(gname), rstd, ALU.mult, ALU.mult)
        P.tt("pool", yo[h % 2], t_y, rz, ALU.mult)
        P.dma(self.YT[row0 + h * 128:row0 + (h + 1) * 128, :], yo[h % 2])


def mix_gdn(self, l, s):
    P = self.P
    ar = self.ar
    ar.reset()
    o16 = ar.h(4, 2048)
    mark0 = ar.off
    kT16 = ar.h(4, 2048)
    kd = ar.h(4, 32 * 128)
    bV = ar.h(4, 32 * 128)
    gam = ar.f(32, 4)
    beta = ar.f(32, 4)
    nbeg = ar.f(32, 4)
    dkl = ar.f(32, 4)
    egl = ar.f(32, 4)
    gtm = ar.f(32, 4)
    S32 = ar.f(4, 128)
    S16 = ar.h(4, 128)
    qT16 = ar.h(4, 2048)
    mark2 = ar.off
    gab = ar.f(32, 8)
    P.dma(gab[0:64], self.PNg.rearrange("(n p) c -> p n c", p=64))
    xs = ar.f(32, 4)
    ax = ar.f(32, 4)
    Aex = ar.f(4)
    dtb_b = self.pv("dtb", 0, 4)[0:64].rearrange("p (o c) -> p o c", o=1).broadcast_to([64, 32, 4])
    P.act(beta[0:64], gab[0:64, :, 4:8], AF.Sigmoid)
    P.tt("dve", xs[0:64], gab[0:64, :, 0:4], dtb_b, ALU.add)
    P.act(ax[0:64], xs[0:64], AF.Abs)
    P.act(ax[0:64], ax[0:64], AF.Exp, scale=-1.0)
    P.act(ax[0:64], ax[0:64], AF.Ln, bias=1.0)
    P.ts("dve", xs[0:64], xs[0:64], 0.0, None, ALU.max)
    P.tt("dve", xs[0:64], xs[0:64], ax[0:64], ALU.add)
    P.act(Aex[0:64], self.pv("alog", 0, 4)[0:64], AF.Exp)
    P.ts("dve", Aex[0:64], Aex[0:64], -1.0, None, ALU.mult)
    P.tt("dve", gtm[0:64], xs[0:64], Aex[0:64].rearrange("p (o c) -> p o c", o=1).broadcast_to([64, 32, 4]), ALU.mult)
    g2 = gtm[0:64].rearrange("p n h -> p (n h)")
    b0 = self.ps[0]
    P.mm(b0[0:64, 0:128], self.c32("TRI", 64, 64), g2, start=True, stop=True)
    P.copy("dve", gam[0:64].rearrange("p n h -> p (n h)"), b0[0:64, 0:128])
    b1 = self.ps[1]
    P.mm(b1[:, 0:128], self.C32[0:64, CO["ONES"]:CO["ONES"] + 128], g2, start=True, stop=True)
    P.act(egl.rearrange("p n h -> p (n h)"), b1[:, 0:128], AF.Exp)
    P.tt("dve", dkl[0:64].rearrange("p n h -> p (n h)"), b1[0:64, 0:128], gam[0:64].rearrange("p n h -> p (n h)"), ALU.subtract)
    P.act(dkl[0:64], dkl[0:64], AF.Exp)
    P.act(nbeg[0:64], gam[0:64], AF.Exp)
    P.tt("dve", nbeg[0:64], nbeg[0:64], beta[0:64], ALU.mult)
    P.ts("dve", nbeg[0:64], nbeg[0:64], -1.0, None, ALU.mult)
    xpad = [ar.f(2051) for _ in range(2)]
    acc = [ar.f(2048) for _ in range(2)]
    sq = ar.h(2048)
    rstd = ar.f(2048)
    tmp = ar.f(512)
    for kind, base in (("k", 4), ("v", 8), ("q", 0)):
        for h in range(4):
            cc = base + h
            xp = xpad[cc % 2]
            a = acc[cc % 2]
            P.memset("pool", xp[:, 0:3], 0.0)
            P.dma(xp[:, 3:2051], _pt_rows(self, {"q": "g_q", "k": "g_k", "v": "g_v"}[kind], h))
            P.act(a, xp[:, 3:2051], AF.Copy, scale=self.pv("gcw", 3 * 12 + cc))
            for j in (2, 1, 0):
                P.stt(a, xp[:, j:j + 2048], self.pv("gcw", j * 12 + cc), a, ALU.mult, ALU.add)
            P.act(a, a, AF.Silu)
            if kind != "v":
                P.act(sq, a, AF.Square)
                for tt in range(4):
                    sl = slice(tt * 512, (tt + 1) * 512)
                    bank = self.ps[tt % 4]
                    P.mm(bank[:, :], self.ONES16, sq[:, sl], start=True, stop=True)
                    _rstd_from(self, rstd[:, sl], bank[:, :], 1, tmp)
            if kind == "q":
                P.stt(qT16[:, h, :], a, 128.0 ** -0.5, rstd, ALU.mult, ALU.mult)
                continue
            if kind == "k":
                P.tt("pool", a, a, rstd, ALU.mult)
                P.copy("act", kT16[:, h, :], a)
            dst = kd if kind == "k" else bV
            sc = dkl if kind == "k" else beta
            for g in range(8):
                bank = self.ps[4 + g % 4]
                for i in range(4):
                    n = g * 4 + i
                    P.transpose(bank[0:64, i * 128:(i + 1) * 128], a[:, n * 64:(n + 1) * 64], self.ID32)
                P.tt("dve", dst[0:64, h, g * 512:(g + 1) * 512].rearrange("p (n d) -> p n d", d=128),
                     bank[0:64, :].rearrange("p (n d) -> p n d", d=128),
                     sc[0:64, g * 4:(g + 1) * 4, h:h + 1].broadcast_to([64, 4, 128]), ALU.mult)
    ar.reset(mark2)
    Tt16 = ar.h(16, 512)
    attnT = ar.h(16, 512)
    qe16 = ar.h(4, 2048)
    mark1 = ar.off
    eG = ar.f(512)
    tD = ar.f(512)
    decT = ar.f(512)
    M0f = ar.f(512)
    if TSOLVE_BF16:
        Mm = [ar.h(512), ar.h(512)]
        Nn = [ar.h(512), ar.h(512)]
        Tt = ar.h(512)
    else:
        Mm = [M0f, ar.f(512)]
        Nn = [ar.f(512), ar.f(512)]
        Tt = ar.f(512)
    bs = ar.f(512)
    NEGM8 = self.c32("NEGM8", 512, 64)
    NEGS8 = self.c32("NEGS8", 512, 64)
    I8 = self.c32("I8", 512, 64)
    TRI = self.c32("TRI", 64, 64)
    ID64 = self.C32[0:64, CO["ID"]:CO["ID"] + 64]
    def v8(t):
        return t[0:64].rearrange("p (g t) -> p g t", t=64)
    for blk in range(16):
        bG, bK, bQ, bB = self.ps[0], self.ps[1], self.ps[2], self.ps[3]
        for ci in range(2):
            n = blk * 2 + ci
            cs = slice(n * 64, (n + 1) * 64)
            for h in range(4):
                col = (ci * 4 + h) * 64
                P.mm(bG[:, col:col + 64], gtm[0:64, n, h:h + 1].broadcast_to([64, 128]), TRI, start=True, stop=True)
                P.mm(bK[0:64, col:col + 64], kT16[:, h, cs], kT16[:, h, cs], start=True, stop=True)
                P.mm(bQ[0:64, col:col + 64], kT16[:, h, cs], qT16[:, h, cs], start=True, stop=True)
                P.mm(bB[0:64, col:col + 64], beta[0:64, n, h:h + 1].broadcast_to([64, 64]), ID64, start=True, stop=True)
        P.act(eG, bG[:, :], AF.Exp)
        gcol = gam[0:64, 2 * blk:2 * blk + 2, :].rearrange("p n h -> p (n h)").rearrange("p (g o) -> p g o", o=1).broadcast_to([64, 8, 64])
        P.tt("dve", v8(tD), v8(bG), gcol, ALU.subtract)
        P.tt("pool", tD[0:64], tD[0:64], NEGM8, ALU.add)
        P.act(decT[0:64], tD[0:64], AF.Exp)
        for ci in range(2):
            n = blk * 2 + ci
            cs = slice(n * 64, (n + 1) * 64)
            P.tt("pool" if ci else "dve", qe16[:, :, cs], qT16[:, :, cs],
                 eG[:, ci * 256:(ci + 1) * 256].rearrange("p (h t) -> p h t", t=64), ALU.mult)
        P.tt("dve", attnT[0:64, blk, :], bQ[0:64, :], decT[0:64], ALU.mult)
        M0, N0 = Mm[0], Nn[0]
        P.tt("dve", M0f[0:64], bK[0:64, :], decT[0:64], ALU.mult)
        P.tt("dve", bs[0:64], bB[0:64, :], NEGS8, ALU.mult)
        P.tt("pool", M0f[0:64], M0f[0:64], bs[0:64], ALU.mult)
        bN = self.ps[4]
        for g in range(8):
            P.transpose(bN[0:64, g * 64:(g + 1) * 64], M0f[0:64, g * 64:(g + 1) * 64], ID64)
        P.copy("act", N0[0:64], bN[0:64, :])
        if TSOLVE_BF16:
            P.copy("pool", M0[0:64], M0f[0:64])
        P.tt("pool", Tt[0:64], M0f[0:64], I8, ALU.add)
        bMh = (self.ps[5], self.ps[0])
        bNh = (self.ps[6], self.ps[1])
        bTh = (self.ps[7], self.ps[2])
        for lev in range(1, 6):
            Mp, Np = Mm[(lev - 1) % 2], Nn[(lev - 1) % 2]
            Mq, Nq = Mm[lev % 2], Nn[lev % 2]
            for half in range(2):
                hs = slice(half * 256, (half + 1) * 256)
                bM, bN2 = bMh[half], bNh[half]
                if lev < 5:
                    for g in range(half * 4, half * 4 + 4):
                        c = slice(g * 64, (g + 1) * 64)
                        P.mm(bM[0:64, c], Np[0:64, c], Mp[0:64, c], start=True, stop=True)
                for g in range(half * 4, half * 4 + 4):
                    c = slice(g * 64, (g + 1) * 64)
                    P.mm(bN2[0:64, c], Mp[0:64, c], Np[0:64, c], start=True, stop=True)
                if lev < 5:
                    P.copy("act", Mq[0:64, hs], bM[0:64, hs])
                P.copy("dve", Nq[0:64, hs], bN2[0:64, hs])
            for half in range(2):
                hs = slice(half * 256, (half + 1) * 256)
                bT = bTh[half]
                for g in range(half * 4, half * 4 + 4):
                    c = slice(g * 64, (g + 1) * 64)
                    P.mm(bT[0:64, c], Nq[0:64, c], Tt[0:64, c], start=True, stop=True)
                P.tt("dve", Tt[0:64, hs], bT[0:64, hs], Tt[0:64, hs], ALU.add)
        P.copy("act", Tt16[0:64, blk, :], Tt[0:64])
        self.pump(3, after=P.ops["pe"][-1])
    ar.reset(mark1)
    R16 = [ar.h(512), ar.h(512)]
    Vn16 = [ar.h(512), ar.h(512)]
    P.memset("dve", S32, 0.0)
    P.memset("pool", S16, 0.0)
    for blk in range(16):
        Ob = self.ps[4 + blk % 2]
        for ci in range(2):
            n = blk * 2 + ci
            cs = slice(n * 64, (n + 1) * 64)
            bKS = self.ps[n % 2]
            bV_ = self.ps[2 + n % 2]
            Sb = self.ps[6 + n % 2]
            R = R16[n % 2]
            Vn = Vn16[n % 2]
            for h in range(4):
                P.mm(bKS[0:64, h * 128:(h + 1) * 128], kT16[:, h, cs], S16[:, h, :], start=True, stop=True)
            for h in range(4):
                P.stt(R[0:64, h * 128:(h + 1) * 128], bKS[0:64, h * 128:(h + 1) * 128], nbeg[0:64, n, h:h + 1],
                      bV[0:64, h, n * 128:(n + 1) * 128], ALU.mult, ALU.add)
            for h in range(4):
                tcol = (ci * 4 + h) * 64
                P.mm(bV_[0:64, h * 128:(h + 1) * 128], Tt16[0:64, blk, tcol:tcol + 64], R[0:64, h * 128:(h + 1) * 128],
                     start=True, stop=True)
            P.copy("act", Vn[0:64], bV_[0:64, :])
            for h in range(4):
                ocol = (h * 2 + ci) * 64
                acol = (ci * 4 + h) * 64
                P.mm(Ob[:, ocol:ocol + 64], S16[:, h, :], qe16[:, h, cs], start=True, stop=False)
                P.mm(Ob[:, ocol:ocol + 64], Vn[0:64, h * 128:(h + 1) * 128], attnT[0:64, blk, acol:acol + 64],
                     start=False, stop=True)
                P.mm(Sb[:, h * 128:(h + 1) * 128], kd[0:64, h, n * 128:(n + 1) * 128], Vn[0:64, h * 128:(h + 1) * 128],
                     start=True, stop=True)
            for h in range(4):
                P.stt(S32[:, h, :], S32[:, h, :], egl[:, n, h:h + 1], Sb[:, h * 128:(h + 1) * 128], ALU.mult, ALU.add)
            P.copy("act", S16, S32)
        P.copy("act", o16[:, :, blk * 128:(blk + 1) * 128], Ob[:, :].rearrange("p (h t) -> p h t", t=128))
    ar.reset(mark0)
    _out_norm_gate(self, o16, "gng", "g_z", AF.Silu, 1024)


Builder.mix_gdn = mix_gdn
```
